# Optimizing a Trainium2 kernel written in Bass

```python
import math
import jax, jax.numpy as jnp
from jax import lax
import numpy as np

D_MODEL = 2048
BATCH = 2
SEQ = 4096
DEPTH = 1

CHUNK = 64
N_MEM = 256
EPS = 1e-6
NEG = -1e30

MLA_HEADS = 8
MLA_NOPE = 128
MLA_ROPE = 64
MLA_V = 128
MLA_QLORA = 512
MLA_KVLORA = 256
MLA_WIDTH = MLA_HEADS * MLA_V
ROPE_THETA = 10000.0
QUERY_BLOCK = 128

SWA_HEADS = 8
SWA_KV_HEADS = 2
SWA_HEAD_DIM = 64
SWA_WIDTH = SWA_HEADS * SWA_HEAD_DIM
SWA_KV_WIDTH = SWA_KV_HEADS * SWA_HEAD_DIM
WINDOW = 128
WINDOW_CHUNKS = WINDOW // CHUNK
SWA_BLOCK = 128

MEM_HEADS = 4
MEM_HEAD_DIM = 128
MEM_WIDTH = MEM_HEADS * MEM_HEAD_DIM

MIX_WIDTH = MLA_WIDTH + SWA_WIDTH + MEM_WIDTH

N_BUCKETS = 32
MAX_DISTANCE = 128

IN_SIZES = (
    MLA_QLORA,
    MLA_KVLORA,
    MLA_ROPE,
    MLA_WIDTH,
    SWA_WIDTH,
    SWA_KV_WIDTH,
    SWA_KV_WIDTH,
    SWA_WIDTH,
    MEM_WIDTH,
    MEM_WIDTH,
)
IN_WIDTH = 4160

kernel_name = "hybrid_mla_swa_sink_memory_block"


def rms_norm(x, g):
    xf = x.astype(jnp.float32)
    y = xf * lax.rsqrt(jnp.mean(xf * xf, axis=-1, keepdims=True) + EPS)
    return (y * g.astype(jnp.float32)).astype(x.dtype)


def split_cols(t, sizes):
    idx = []
    acc = 0
    for s in sizes[:-1]:
        acc += s
        idx.append(acc)
    return jnp.split(t, idx, axis=-1)


def rope_tables(seq, dim):
    inv = 1.0 / (ROPE_THETA ** (jnp.arange(0, dim, 2, dtype=jnp.float32) / dim))
    ang = jnp.arange(seq, dtype=jnp.float32)[:, None] * inv[None, :]
    return jnp.cos(ang), jnp.sin(ang)


def apply_rope(x, cos, sin):
    half = x.shape[-1] // 2
    x1 = x[..., :half].astype(jnp.float32)
    x2 = x[..., half:].astype(jnp.float32)
    out = jnp.concatenate([x1 * cos - x2 * sin, x2 * cos + x1 * sin], axis=-1)
    return out.astype(x.dtype)


def t5_bucket(rel):
    nb = N_BUCKETS // 2
    max_exact = nb // 2
    bucket = jnp.where(rel > 0, nb, 0)
    n = jnp.abs(rel)
    nf = jnp.maximum(n, 1).astype(jnp.float32)
    large = max_exact + (jnp.log(nf / max_exact) / math.log(MAX_DISTANCE / max_exact)
                         * (nb - max_exact)).astype(jnp.int32)
    large = jnp.minimum(large, nb - 1)
    return bucket + jnp.where(n < max_exact, n, large)


def mla_attention(c_q, c_kv, k_pe_in, g_q, g_kv, w_uq, w_ukv, cos, sin):
    B, S, _ = c_q.shape
    q = (rms_norm(c_q, g_q) @ w_uq).reshape(B, S, MLA_HEADS, MLA_NOPE + MLA_ROPE)
    q_nope, q_pe = q[..., :MLA_NOPE], q[..., MLA_NOPE:]
    q_pe = apply_rope(q_pe, cos[:, None, :], sin[:, None, :])
    kv = (rms_norm(c_kv, g_kv) @ w_ukv).reshape(B, S, MLA_HEADS, MLA_NOPE + MLA_V)
    k_nope, v = kv[..., :MLA_NOPE], kv[..., MLA_NOPE:]
    k_pe = apply_rope(k_pe_in, cos, sin)
    scale = (MLA_NOPE + MLA_ROPE) ** -0.5
    nblk = S // QUERY_BLOCK
    qn_b = q_nope.reshape(B, nblk, QUERY_BLOCK, MLA_HEADS, MLA_NOPE).transpose(1, 0, 2, 3, 4)
    qp_b = q_pe.reshape(B, nblk, QUERY_BLOCK, MLA_HEADS, MLA_ROPE).transpose(1, 0, 2, 3, 4)
    k_chunk = jnp.arange(S) // CHUNK

    def block(args):
        qn, qp, b = args
        q_chunk = (b * QUERY_BLOCK + jnp.arange(QUERY_BLOCK)) // CHUNK
        s = (jnp.einsum('bqhd,bkhd->bhqk', qn, k_nope)
             + jnp.einsum('bqhd,bkd->bhqk', qp, k_pe)).astype(jnp.float32) * scale
        mask = k_chunk[None, :] <= q_chunk[:, None]
        s = jnp.where(mask[None, None], s, NEG)
        p = jax.nn.softmax(s, axis=-1).astype(v.dtype)
        return jnp.einsum('bhqk,bkhd->bqhd', p, v)

    out = lax.map(block, (qn_b, qp_b, jnp.arange(nblk)))
    return out.transpose(1, 0, 2, 3, 4).reshape(B, S, MLA_WIDTH)


def swa_attention(q, k, v, sinks, rel_table):
    B, S, _ = q.shape
    nblk = S // SWA_BLOCK
    G = SWA_HEADS // SWA_KV_HEADS
    q = q.reshape(B, nblk, SWA_BLOCK, SWA_KV_HEADS, G, SWA_HEAD_DIM)

    def band(t):
        t = t.reshape(B, S, SWA_KV_HEADS, SWA_HEAD_DIM)
        t = jnp.pad(t, ((0, 0), (SWA_BLOCK, 0), (0, 0), (0, 0)))
        t = t.reshape(B, nblk + 1, SWA_BLOCK, SWA_KV_HEADS, SWA_HEAD_DIM)
        return jnp.concatenate([t[:, :-1], t[:, 1:]], axis=2)

    k_band, v_band = band(k), band(v)
    s = jnp.einsum('bnqkgd,bnjkd->bnkgqj', q, k_band).astype(jnp.float32) * (SWA_HEAD_DIM ** -0.5)
    qi = jnp.arange(SWA_BLOCK)
    kj = jnp.arange(2 * SWA_BLOCK)
    rel = kj[None, :] - SWA_BLOCK - qi[:, None]
    bias = rel_table.astype(jnp.float32)[t5_bucket(rel)]
    bias = bias.transpose(2, 0, 1).reshape(SWA_KV_HEADS, G, SWA_BLOCK, 2 * SWA_BLOCK)
    s = s + bias
    blk = jnp.arange(nblk)[:, None, None]
    q_pos = blk * SWA_BLOCK + qi[None, :, None]
    k_pos = (blk - 1) * SWA_BLOCK + kj[None, None, :]
    q_c = q_pos // CHUNK
    k_c = k_pos // CHUNK
    valid = (k_pos >= 0) & (k_c <= q_c) & (k_c >= q_c - WINDOW_CHUNKS)
    s = jnp.where(valid[None, :, None, None], s, NEG)
    sink = sinks.astype(jnp.float32).reshape(SWA_KV_HEADS, G)[None, None, :, :, None, None]
    m = jnp.maximum(jnp.max(s, axis=-1, keepdims=True), sink)
    p = jnp.exp(s - m)
    p = p / (jnp.sum(p, axis=-1, keepdims=True) + jnp.exp(sink - m))
    o = jnp.einsum('bnkgqj,bnjkd->bnqkgd', p.astype(v_band.dtype), v_band)
    return o.reshape(B, S, SWA_WIDTH)


def memory_attention(q, mem_n, w_mem_kv):
    B, S, _ = q.shape
    M = mem_n.shape[1]
    kv = mem_n @ w_mem_kv
    k = kv[..., :MEM_WIDTH].reshape(B, M, MEM_HEADS, MEM_HEAD_DIM)
    v = kv[..., MEM_WIDTH:].reshape(B, M, MEM_HEADS, MEM_HEAD_DIM)
    q = q.reshape(B, S, MEM_HEADS, MEM_HEAD_DIM)
    s = jnp.einsum('bshd,bmhd->bhsm', q, k).astype(jnp.float32) * (MEM_HEAD_DIM ** -0.5)
    p = jax.nn.softmax(s, axis=-1).astype(v.dtype)
    return jnp.einsum('bhsm,bmhd->bshd', p, v).reshape(B, S, MEM_WIDTH)


def hybrid_layer(x, mem, g_in, w_in, g_q, g_kv, w_uq, w_ukv, sinks, rel_table,
                 g_mem, w_mem_kv, w_out, cos, sin):
    h = rms_norm(x, g_in)
    (c_q, c_kv, k_pe, z_mla, q_swa, k_swa, v_swa, z_swa, q_mem, z_mem) = split_cols(h @ w_in, IN_SIZES)
    y_mla = mla_attention(c_q, c_kv, k_pe, g_q, g_kv, w_uq, w_ukv, cos, sin) * jax.nn.silu(z_mla)
    y_swa = swa_attention(q_swa, k_swa, v_swa, sinks, rel_table) * jax.nn.silu(z_swa)
    y_mem = memory_attention(q_mem, rms_norm(mem, g_mem), w_mem_kv) * jax.nn.silu(z_mem)
    y = jnp.concatenate([y_mla, y_swa, y_mem], axis=-1) @ w_out
    return x + y


def setup_inputs(seed: int = 0) -> dict:
    key = jax.random.key(seed)
    ks = jax.random.split(key, 16)
    f32 = jnp.float32

    def nrm(k, shape, scale):
        return jax.random.normal(k, shape, f32) * scale

    def gain(k, shape):
        return 1.0 + 0.05 * jax.random.normal(k, shape, f32)

    return {
        "x": jax.random.normal(ks[0], (BATCH, SEQ, D_MODEL), f32),
        "mem": jax.random.normal(ks[1], (BATCH, N_MEM, D_MODEL), f32),
        "norm_in": gain(ks[2], (DEPTH, D_MODEL)),
        "w_in": nrm(ks[3], (DEPTH, D_MODEL, IN_WIDTH), D_MODEL ** -0.5),
        "norm_q": gain(ks[4], (DEPTH, MLA_QLORA)),
        "norm_kv": gain(ks[5], (DEPTH, MLA_KVLORA)),
        "w_uq": nrm(ks[6], (DEPTH, MLA_QLORA, MLA_HEADS * (MLA_NOPE + MLA_ROPE)), MLA_QLORA ** -0.5),
        "w_ukv": nrm(ks[7], (DEPTH, MLA_KVLORA, MLA_HEADS * (MLA_NOPE + MLA_V)), MLA_KVLORA ** -0.5),
        "attn_sinks": nrm(ks[8], (DEPTH, SWA_HEADS), 0.5),
        "rel_bias": nrm(ks[9], (N_BUCKETS, SWA_HEADS), 0.5),
        "norm_mem": gain(ks[10], (DEPTH, D_MODEL)),
        "w_mem_kv": nrm(ks[11], (DEPTH, D_MODEL, 2 * MEM_WIDTH), D_MODEL ** -0.5),
        "w_out": nrm(ks[12], (DEPTH, MIX_WIDTH, D_MODEL), MIX_WIDTH ** -0.5),
        "norm_final": gain(ks[13], (D_MODEL,)),
    }


def reference(x, mem, norm_in, w_in, norm_q, norm_kv, w_uq, w_ukv, attn_sinks, rel_bias,
              norm_mem, w_mem_kv, w_out, norm_final):
    cos, sin = rope_tables(x.shape[1], MLA_ROPE)
    h = x
    for l in range(DEPTH):
        h = hybrid_layer(h, mem, norm_in[l], w_in[l], norm_q[l], norm_kv[l], w_uq[l], w_ukv[l],
                         attn_sinks[l], rel_bias, norm_mem[l], w_mem_kv[l], w_out[l], cos, sin)
    return rms_norm(h, norm_final)
```

```python
import contextlib
import numpy as np
import concourse.bass as bass
import concourse.mybir as mybir
from concourse.bass_utils import run_bass_kernel_spmd

F32 = mybir.dt.float32
BF16 = mybir.dt.bfloat16
AF = mybir.ActivationFunctionType
ALU = mybir.AluOpType

D = 2048
S = 4096
NB = 2
KB = 1024
SB0 = 16512
EPS = 1e-6
BIG = 30000.0
MLA_SCALE = 192.0 ** -0.5
MEM_SCALE = 128.0 ** -0.5

C_CQ, C_CKV, C_KPE, C_ZMLA, C_QSWA, C_KSWA, C_VSWA, C_ZSWA, C_QMEM, C_ZMEM = (
    0, 512, 768, 832, 1856, 2368, 2496, 2624, 3136, 3648)


class Buf:
    __slots__ = ("t", "w", "r", "dkey", "dcnt", "ps")


class Prog:
    ENG = ("pe", "act", "dve", "pool", "sp")
    CE = ("pe", "act", "dve", "pool")

    def __init__(self, nc, es):
        self.nc = nc
        self.es = es
        self.ops = {e: [] for e in self.ENG}
        self.cnt = {e: 0 for e in self.CE}
        self.known = {e: {} for e in self.ENG}
        self.sem = {}
        for e in self.CE:
            self.sem[e] = es.enter_context(nc.semaphore("sem_" + e))
        self.nd = 0
        self.nt = 0
        self.label = ""
        self.labels = {e: [] for e in self.ENG}
        self.oplabels = {e: [] for e in self.ENG}

    def sb(self, shape, dtype, off, fresh=False):
        self.nt += 1
        t = self.nc.alloc_sbuf_tensor_at("sb%d" % self.nt, list(shape), dtype, offset=SB0 + off)
        return self.mk(t, fresh)

    def mk(self, t, fresh=True):
        b = Buf()
        b.t = t
        b.w = [] if fresh else [(e, c) for e, c in self.cnt.items() if c > 0]
        b.r = []
        b.dkey = None
        b.dcnt = 0
        b.ps = False
        return b

    def _waits(self, eng, reads, writes, skip=None):
        need = {}

        def add(ev):
            k, v = ev
            if k == skip:
                return
            if v > need.get(k, 0):
                need[k] = v
        for b in reads:
            for ev in b.w:
                add(ev)
        for b in writes:
            for ev in b.w:
                add(ev)
            for ev in b.r:
                add(ev)
        kn = self.known[eng]
        wl = []
        for k, v in need.items():
            if eng == "pe" and k == "pe":
                continue
            if kn.get(k, 0) >= v:
                continue
            kn[k] = v
            wl.append((k, v))
        return wl

    def op(self, eng, fn, reads=(), writes=()):
        psr = [b for b in reads if b.ps]
        if psr:
            reads = [b for b in reads if not b.ps]
            writes = list(writes) + [b for b in psr if b not in writes]
        wl = self._waits(eng, reads, writes)
        self.cnt[eng] += 1
        ev = (eng, self.cnt[eng])
        self.ops[eng].append((wl, fn, eng))
        self.oplabels[eng].append(self.label)
        for b in reads:
            b.r.append(ev)
        for b in writes:
            b.w = [ev]
            b.r = []
        return ev

    def dma(self, q, out_ap, in_ap, owner, reads=(), writes=()):
        if owner.dkey is None:
            owner.dkey = {}
            owner.dcnt = {}
        if q not in owner.dkey:
            self.nd += 1
            owner.dkey[q] = ("d", self.nd)
            owner.dcnt[q] = 0
            self.sem[owner.dkey[q]] = self.es.enter_context(self.nc.semaphore("dsem%d" % self.nd))
        key = owner.dkey[q]
        wl = self._waits(q, reads, writes, skip=key)
        owner.dcnt[q] += 16
        ev = (key, owner.dcnt[q])
        self.ops[q].append((wl, lambda e: e.dma_start(out=out_ap, in_=in_ap), key))
        self.oplabels[q].append(self.label + "/dma")
        for b in reads:
            b.r.append(ev)
        for b in writes:
            b.w = [ev]
            b.r = []
        return ev

    def totals(self, owner):
        return [(owner.dkey[q], owner.dcnt[q]) for q in (owner.dkey or {})]

    def replay(self, eng, handle):
        class _Cnt:
            def __init__(s_, h):
                s_.h = h
                s_.n = 0

            def __getattr__(s_, name):
                a = getattr(s_.h, name)
                if callable(a):
                    def w(*args, **kw):
                        s_.n += 1
                        return a(*args, **kw)
                    return w
                return a
        for (wl, fn, sig), lab in zip(self.ops[eng], self.oplabels[eng]):
            for k, v in wl:
                handle.wait_ge(self.sem[k], v)
            c = _Cnt(handle)
            ins = fn(c)
            self.labels[eng].append((lab, c.n, [(str(k), v) for k, v in wl]))
            if sig is None:
                continue
            if isinstance(sig, tuple):
                ins.then_inc(self.sem[sig], 16)
            else:
                ins.then_inc(self.sem[sig], 1)


class _Stop(Exception):
    pass


def build(stop=99):
    nc = bass.Bass("TRN2", target_bir_lowering=False)
    es = contextlib.ExitStack()
    P = Prog(nc, es)

    def din(name, shape):
        return nc.dram_tensor(name, list(shape), F32, kind="ExternalInput").ap()

    xp = din("xp", [S + 512, D])
    memx = din("memx", [256, D])
    csk = din("csk", [128, S])
    ohk = din("ohk", [64, S])
    gtm = din("gtm", [64, 1024])
    w_in = din("w_in", [D, 4160])
    w_uq = din("w_uq", [512, 1536])
    w_ukv = din("w_ukv", [256, 2048])
    w_mkv = din("w_mkv", [D, 1024])
    w_out = din("w_out", [D, D])
    gin_d = din("gin", [128, D])
    gmem_d = din("gmem", [128, D])
    gfin_d = din("gfin", [128, D])
    gq_d = din("gq", [128, 4])
    gkv_d = din("gkv", [128, 2])
    swab_d = din("swab", [128, 2 * 8 * 128])
    swam_d = din("swam", [128, 2 * 128])
    kvb_d = din("kvb", [128, 8])
    sink_d = din("sinkr", [1, 1024])
    ident_d = din("ident", [128, 128])
    out_d = nc.dram_tensor("out", [1024, D], F32, kind="ExternalOutput").ap()

    DB = [es.enter_context(nc.psum_tensor("psd%d" % i, [128, 1024], F32)) for i in range(4)]
    PS = [P.mk(None) for _ in range(8)]
    for b_ in PS:
        b_.ps = True

    def pf(b, c0, c1, p0=0, p1=128):
        o = (b % 2) * 512
        return DB[b // 2][p0:p1, o + c0:o + c1]

    def pb2(k, c0, c1):
        return DB[k][:].bitcast(BF16)[:, c0:c1]

    def pb(b, c0, c1):
        o = (b % 2) * 1024
        return DB[b // 2][:].bitcast(BF16)[:, o + c0:o + c1]

    o = 0
    ident = P.sb([128, 128], BF16, o, True); o += 256
    ones = P.sb([128, 128], BF16, o, True); o += 256
    gq = P.sb([128, 4], F32, o, True); o += 32
    gkv = P.sb([128, 2], F32, o, True); o += 32
    kvb = P.sb([128, 8], F32, o, True); o += 32
    ssv = [P.sb([128, 4], F32, o + 32 * i, True) for i in range(4)]; o += 128
    assert o <= 1 * KB
    R0 = 1 * KB
    kvnT = P.sb([128, 2, S], BF16, R0, True)
    kpeT = P.sb([128, S], BF16, R0 + 16 * KB, True)
    R1 = R0 + 24 * KB
    RING = R1 + 64 * KB
    ring = [P.sb([128, 16, 256], BF16, RING + 8 * KB * i, True) for i in range(3)]
    R5 = RING + 24 * KB
    hT_own = P.sb([128, 16, 1024], BF16, R5, True)
    hT_own_hi = P.mk(hT_own.t)
    R6 = R5 + 32 * KB
    kk0 = P.sb([128, 12 * 128], BF16, R6, True)
    kk1 = P.sb([128, 12 * 128], BF16, R6 + 3 * KB, True)
    vv = P.sb([128, 12, 256], BF16, R6 + 6 * KB, True)
    R6b = R6 + 12 * KB
    R7 = R6b + 16 * KB
    memT = P.sb([128, 16, 256], BF16, 199 * KB, True)
    memT_hi = P.mk(memT.t)

    wkv = P.sb([128, 16, 640], BF16, R1, True)
    gin = P.sb([128, D], F32, R1 + 20 * KB, True)
    xt = [P.sb([128, D], F32, R1 + 28 * KB + 8 * KB * i, True) for i in range(4)]
    hh = [P.sb([128, D], BF16, R1 + 60 * KB, True), P.sb([128, D], BF16, R6b, True),
          P.sb([128, D], BF16, R6b + 4 * KB, True)]
    hT_rot = [P.sb([128, 16, 256], BF16, R6b + 8 * KB, True), P.sb([128, 16, 256], BF16, R6b + 16 * KB, True)]
    hT_rot_hi = [P.mk(hT_rot[0].t), P.mk(hT_rot[1].t)]
    HI = {id(hT_rot[0]): hT_rot_hi[0], id(hT_rot[1]): hT_rot_hi[1], id(hT_own): hT_own_hi, id(memT): memT_hi}
    a_sc = R6b + 24 * KB
    sq = [P.sb([128, 2, 256], BF16, a_sc + KB * i, True) for i in range(2)]
    rstd2 = [P.sb([128, 256], F32, a_sc + 2 * KB + KB * i, True) for i in range(2)]
    tt_ = [P.sb([128, 256], F32, a_sc + 4 * KB, True)] * 2
    t2_ = P.sb([128, 256], F32, a_sc + 6 * KB, True)
    cst = [P.sb([128, 256], F32, a_sc + 7 * KB + KB * i, True) for i in range(2)]
    vvTs = [P.sb([128, 2, 256], BF16, a_sc + 9 * KB, True), P.sb([128, 2, 256], BF16, a_sc + 5 * KB, True)]
    gmem = P.sb([128, D], F32, 191 * KB, True)

    OUTS = [P.mk(None), P.mk(None)]

    def chk(n):
        if stop <= n:
            raise _Stop()

    def record():
        def wv(c0, c1):
            return w_in[:, c0:c1].rearrange("(kc p) c -> p kc c", p=128)
        CONA = P.mk(None)
        wkv_sw = P.mk(wkv.t)
        P.dma("pool", ident.t[:], ident_d[:, :], CONA)
        P.dma("sp", gin.t[:], gin_d[:, :], CONA)
        P.dma("sp", gkv.t[:], gkv_d[:, :], CONA)
        for b in (ident, gin, gkv):
            b.w = P.totals(CONA)
        CONA2 = P.mk(None)
        for (dst, src, n) in [(0, C_CKV, 256), (256, C_KPE, 64), (320, C_KPE + 32, 32), (352, C_KPE, 32)]:
            P.dma("pool", wkv.t[:, :, dst:dst + n], wv(src, src + n), CONA2)
        wkv.w = P.totals(CONA2)

        conb_tot = []

        def load_conb():
            CONB = P.mk(None)
            P.dma("sp", gq.t[:], gq_d[:, :], CONB)
            P.dma("sp", kvb.t[:], kvb_d[:, :], CONB)
            P.dma("sp", gmem.t[:], gmem_d[:, :], CONB)
            for (dst, src, n) in [(384, C_KSWA, 128), (512, C_VSWA, 128)]:
                P.dma("pool", wkv.t[:, :, dst:dst + n], wv(src, src + n), CONB)
            P.dma("pool", kpeT.t[64:128, :], ohk[:, :], CONB)
            for b in (gq, kvb, gmem, wkv_sw):
                b.w = P.totals(CONB)
            kpeT.w = kpeT.w + P.totals(CONB)
            conb_tot.extend(P.totals(CONB))
        P.op("pool", lambda e: e.memset(ones.t[:], 1.0), writes=[ones])
        chk(1)

        ring_i = [0]

        def ring_load(src_ap):
            b = ring[ring_i[0] % 3]
            ring_i[0] += 1
            P.dma("pool", b.t[:], src_ap, b, writes=[b])
            return b

        NT = 38

        def tinfo(t):
            u, tl = t // 2, t % 2
            if u < 12:
                hTb, c0 = hT_rot[u % 2], 0
            elif u < 16:
                hTb, c0 = hT_own, (u - 12) * 256
            elif u < 18:
                hTb, c0 = hT_rot[u % 2], 0
            else:
                hTb, c0 = memT, 0
            if u < 18:
                src, gb = xp[u * 256 + tl * 128:u * 256 + (tl + 1) * 128, :], gin
            else:
                src, gb = memx[tl * 128:(tl + 1) * 128, :], gmem
            return u, tl, hTb, c0 + tl * 128, src, gb

        def S0(t):
            u, tl, hTb, c0, src, gb = tinfo(t)
            x_ = xt[t % 4]
            P.dma("sp", x_.t[:], src, x_, writes=[x_])

        def S1(t):
            x_, h_, sv = xt[t % 4], hh[t % 3], ssv[t % 3]
            P.op("act", lambda e: e.activation(out=h_.t[:], in_=x_.t[:], func=AF.Square, accum_out=sv.t[:, 0:1]),
                 reads=[x_], writes=[h_, sv])
            P.op("act", lambda e: e.activation(out=sv.t[:, 1:2], in_=sv.t[:, 0:1], func=AF.Ln, bias=EPS, scale=1.0 / D),
                 reads=[sv], writes=[sv])
            P.op("act", lambda e: e.activation(out=sv.t[:, 2:3], in_=sv.t[:, 1:2], func=AF.Exp, scale=-0.5),
                 reads=[sv], writes=[sv])

        def S2(t):
            u, tl, hTb, c0, src, gb = tinfo(t)
            x_, h_, sv = xt[t % 4], hh[t % 3], ssv[t % 3]
            P.op("dve", lambda e: e.scalar_tensor_tensor(out=h_.t[:], in0=x_.t[:], scalar=sv.t[:, 2:3], in1=gb.t[:],
                                                         op0=ALU.mult, op1=ALU.mult),
                 reads=[x_, sv, gb], writes=[h_])

        def S3(t):
            h_ = hh[t % 3]
            k = t % 2

            def tr(e):
                ins = None
                for kc in range(16):
                    ins = e.transpose(pb2(k, kc * 128, (kc + 1) * 128), h_.t[:, kc * 128:(kc + 1) * 128], ident.t[:])
                return ins
            P.op("pe", tr, reads=[h_, ident], writes=[PS[2 * k], PS[2 * k + 1]])

        def S4(t):
            u, tl, hTb, c0, src, gb = tinfo(t)
            k = t % 2
            P.op("act", lambda e: e.activation(out=hTb.t[:, 0:8, c0:c0 + 128],
                                               in_=pb2(k, 0, 1024).rearrange("p (a b) -> p a b", b=128), func=AF.Copy),
                 reads=[PS[2 * k]], writes=[hTb])
            P.op("dve", lambda e: e.tensor_copy(out=hTb.t[:, 8:16, c0:c0 + 128],
                                                in_=pb2(k, 1024, 2048).rearrange("p (a b) -> p a b", b=128)),
                 reads=[PS[2 * k + 1]], writes=[HI[id(hTb)]])

        def uinfo(u):
            if u < 12:
                return hT_rot[u % 2], 0
            if u < 16:
                return hT_own, (u - 12) * 256
            return hT_rot[u % 2], 0

        def U1(u):
            hTb, c0 = uinfo(u)
            p0, p1 = 4 + 2 * (u % 2), 5 + 2 * (u % 2)
            cs_ = cst[u % 2]
            P.dma("sp", cs_.t[:], csk[:, u * 256:(u + 1) * 256], cs_, writes=[cs_])

            def mm(e):
                ins = None
                for g in range(3):
                    dst = pf(p0, g * 256, (g + 1) * 256) if g < 2 else pf(p1, 0, 256)
                    for kc in range(16):
                        ins = e.matmul(dst, lhsT=wkv.t[:, kc, g * 128:(g + 1) * 128], rhs=hTb.t[:, kc, c0:c0 + 256],
                                       start=(kc == 0), stop=(kc == 15))
                return ins
            P.op("pe", mm, reads=[wkv, hTb, HI[id(hTb)]], writes=[PS[p0], PS[p1]])

        def U2(u):
            p0 = 4 + 2 * (u % 2)
            sq_ = sq[u % 2]
            P.op("act", lambda e: e.activation(out=sq_.t[:].rearrange("p a b -> p (a b)"), in_=pf(p0, 0, 512), func=AF.Square),
                 reads=[PS[p0]], writes=[sq_])

        def U3(u):
            p1 = 5 + 2 * (u % 2)
            sq_ = sq[u % 2]

            def mm2(e):
                e.matmul(pf(p1, 256, 512), lhsT=ones.t[:], rhs=sq_.t[:, 0, :], start=True, stop=False)
                return e.matmul(pf(p1, 256, 512), lhsT=ones.t[:], rhs=sq_.t[:, 1, :], start=False, stop=True)
            P.op("pe", mm2, reads=[ones, sq_], writes=[PS[p1]])

        def U4(u):
            p1 = 5 + 2 * (u % 2)
            r_ = rstd2[u % 2]
            P.op("act", lambda e: e.activation(out=r_.t[:], in_=pf(p1, 256, 512), func=AF.Ln, bias=EPS, scale=1.0 / 256),
                 reads=[PS[p1]], writes=[r_])
            P.op("act", lambda e: e.activation(out=r_.t[:], in_=r_.t[:], func=AF.Exp, scale=-0.5),
                 reads=[r_], writes=[r_])

        def U5(u):
            p0, p1 = 4 + 2 * (u % 2), 5 + 2 * (u % 2)
            r_, cs_, t_ = rstd2[u % 2], cst[u % 2], tt_[u % 2]
            for c in range(2):
                P.op("dve", lambda e, c=c: e.scalar_tensor_tensor(
                    out=kvnT.t[:, c, u * 256:(u + 1) * 256], in0=pf(p0, c * 256, (c + 1) * 256), scalar=gkv.t[:, c:c + 1],
                    in1=r_.t[:], op0=ALU.mult, op1=ALU.mult), reads=[PS[p0], gkv, r_], writes=[kvnT])
            P.op("dve", lambda e: e.tensor_tensor(out=t_.t[:], in0=pf(p1, 0, 256), in1=cs_.t[:], op=ALU.mult),
                 reads=[PS[p1], cs_], writes=[t_])
            P.op("pool", lambda e: e.tensor_copy(out=t2_.t[0:64, :], in_=t_.t[64:128, :]), reads=[t_], writes=[t2_])
            P.op("pool", lambda e: e.tensor_tensor(out=kpeT.t[0:64, u * 256:(u + 1) * 256], in0=t_.t[0:64, :],
                                                   in1=t2_.t[0:64, :], op=ALU.add), reads=[t_, t2_], writes=[kpeT])

        def SW(u):
            hTb, c0 = uinfo(u)
            p0, p1 = 4 + 2 * (u % 2), 5 + 2 * (u % 2)
            t0 = (u - 12) * 2 if u < 16 else 8 + (u - 16) * 2
            vvT = vvTs[u % 2]

            def mm3(e):
                ins = None
                for g in range(2):
                    dst = pf(p0 + g, 0, 256)
                    for kc in range(16):
                        ins = e.matmul(dst, lhsT=wkv.t[:, kc, (3 + g) * 128:(4 + g) * 128], rhs=hTb.t[:, kc, c0:c0 + 256],
                                       start=(kc == 0), stop=(kc == 15))
                return ins
            P.op("pe", mm3, reads=[wkv, wkv_sw, hTb, HI[id(hTb)]], writes=[PS[p0], PS[p1]])
            cs_ = slice(t0 * 128, (t0 + 2) * 128)
            for (dstb, r0) in ((kk0, 0), (kk1, 64)):
                for o0 in (0, 64):
                    P.op("dve", lambda e, dstb=dstb, r0=r0, o0=o0: e.tensor_copy(out=dstb.t[o0:o0 + 64, cs_], in_=pf(p0, 0, 256, r0, r0 + 64)),
                         reads=[PS[p0]], writes=[dstb])
            for (jj, r0) in ((0, 0), (1, 64)):
                for o0 in (0, 64):
                    P.op("act", lambda e, jj=jj, r0=r0, o0=o0: e.activation(out=vvT.t[o0:o0 + 64, jj, :], in_=pf(p1, 0, 256, r0, r0 + 64), func=AF.Copy),
                         reads=[PS[p1]], writes=[vvT])

        def SWb(u):
            p0 = 4 + 2 * (u % 2)
            t0 = (u - 12) * 2 if u < 16 else 8 + (u - 16) * 2
            vvT = vvTs[u % 2]

            def tr2(e):
                ins = None
                for tl in range(2):
                    for jj in range(2):
                        ins = e.transpose(pb(p0, (tl * 2 + jj) * 128, (tl * 2 + jj + 1) * 128),
                                          vvT.t[:, jj, tl * 128:(tl + 1) * 128], ident.t[:])
                return ins
            P.op("pe", tr2, reads=[vvT, ident], writes=[PS[p0]])
            P.op("dve", lambda e: e.tensor_copy(out=vv.t[:, t0:t0 + 2, :], in_=pb(p0, 0, 512).rearrange("p (a b) -> p a b", b=256)),
                 reads=[PS[p0]], writes=[vv])

        def unit_of(tlast):
            if tlast < 0 or tlast >= NT or tlast % 2 == 0:
                return None
            return tlast // 2

        def L(f, a):
            P.label = "%s(%d)" % (f.__name__, a)
            f(a)

        for k in range(-4, NT + 10):
            if 0 <= k + 4 < NT:
                L(S0, k + 4)
            if k == 8:
                P.label = "conb"
                load_conb()
            u = unit_of(k - 3)
            if u is not None and u < 16:
                L(U2, u)
            u = unit_of(k - 4)
            if u is not None and u < 16:
                L(U4, u)
                L(U5, u)
            if 0 <= k - 1 < NT:
                L(S4, k - 1)
            if 0 <= k + 2 < NT:
                L(S1, k + 2)
            if 0 <= k + 1 < NT:
                L(S2, k + 1)
            if 0 <= k < NT:
                L(S3, k)
            u = unit_of(k - 3)
            if u is not None and u < 16:
                L(U3, u)
            u = unit_of(k - 2)
            if u is not None and u < 16:
                L(U1, u)
            if u is not None and 16 <= u < 18:
                L(SW, u)
            u = unit_of(k - 3)
            if u is not None and 16 <= u < 18:
                L(SWb, u)
            u = unit_of(k - 4)
            if u is not None and 12 <= u < 16:
                L(SW, u)
            u = unit_of(k - 5)
            if u is not None and 12 <= u < 16:
                L(SWb, u)
        chk(4)
        P.label = "own"
        gate = P.sb([128, 16, 1024], BF16, R1)
        cqT = P.sb([128, 4, 1024], BF16, R1 + 32 * KB)
        wuq = P.sb([128, 4, 2048], BF16, R1 + 40 * KB)
        wukv = P.sb([128, 2, 2048], BF16, R1 + 56 * KB)
        qsw = P.sb([128, 4, 1024], BF16, R6b)
        qmem = P.sb([128, 4, 1024], BF16, R6b + 8 * KB)
        cqf = P.sb([128, 4, 512], F32, R7)
        sqq = P.sb([128, 4, 512], BF16, R7 + 8 * KB)
        lnq = P.sb([128, 512], F32, R7 + 12 * KB)

        otiles = [("cq", C_CQ), ("cq", C_CQ + 256)]
        otiles += [("z", C_ZMLA + 256 * i, 2 * i) for i in range(4)]
        otiles += [("qsw", C_QSWA, 0), ("qsw", C_QSWA + 256, 2)]
        otiles += [("z", C_ZSWA, 8), ("z", C_ZSWA + 256, 10)]
        otiles += [("qmem", C_QMEM, 0), ("qmem", C_QMEM + 256, 2)]
        otiles += [("z", C_ZMEM, 12), ("z", C_ZMEM + 256, 14)]
        pend = [ring_load(wv(t[1], t[1] + 256)) for t in otiles[:3]]
        chk(4.05)
        nxt = 3
        obank = [0]

        def proj_group(wb, g, half):
            b = obank[0] % 6
            obank[0] += 1

            def mm(e):
                ins = None
                for kc in range(16):
                    ins = e.matmul(pf(b, 0, 512), lhsT=wb.t[:, kc, g * 128:(g + 1) * 128],
                                   rhs=hT_own.t[:, kc, half * 512:(half + 1) * 512], start=(kc == 0), stop=(kc == 15))
                return ins
            P.op("pe", mm, reads=[wb, hT_own, hT_own_hi], writes=[PS[b]])
            return b

        wq0, wq1 = pend[0], pend[1]
        for half in range(2):
            for c in range(4):
                wb = wq0 if c < 2 else wq1
                b = proj_group(wb, c % 2, half)
                P.op("dve", lambda e, b=b, c=c: e.tensor_copy(out=cqf.t[:, c, :], in_=pf(b, 0, 512)), reads=[PS[b]], writes=[cqf])
                P.op("act", lambda e, c=c: e.activation(out=sqq.t[:, c, :], in_=cqf.t[:, c, :], func=AF.Square),
                     reads=[cqf], writes=[sqq])
                chk(4.1 + 0.01 * c + 0.04 * half)

            def mmq(e):
                ins = None
                for c in range(4):
                    ins = e.matmul(pf(6, 0, 512), lhsT=ones.t[:], rhs=sqq.t[:, c, :], start=(c == 0), stop=(c == 3))
                return ins
            P.op("pe", mmq, reads=[ones, sqq], writes=[PS[6]])
            chk(4.15 + 0.04 * half)
            P.op("act", lambda e: e.activation(out=lnq.t[:], in_=pf(6, 0, 512), func=AF.Ln, bias=EPS, scale=1.0 / 512),
                 reads=[PS[6]], writes=[lnq])
            P.op("act", lambda e: e.activation(out=lnq.t[:], in_=lnq.t[:], func=AF.Exp, scale=-0.5), reads=[lnq], writes=[lnq])
            for c in range(4):
                P.op("dve", lambda e, c=c, half=half: e.scalar_tensor_tensor(
                    out=cqT.t[:, c, half * 512:(half + 1) * 512], in0=cqf.t[:, c, :], scalar=gq.t[:, c:c + 1], in1=lnq.t[:],
                    op0=ALU.mult, op1=ALU.mult), reads=[cqf, gq, lnq], writes=[cqT])
        chk(4.2)
        pend = pend[2:]
        while len(pend) < 3 and nxt < len(otiles):
            t = otiles[nxt]; nxt += 1
            pend.append(ring_load(wv(t[1], t[1] + 256)))
        ev_i = [0]
        for ti in range(2, len(otiles)):
            kind, col, ch = otiles[ti]
            wb = pend.pop(0)
            for g in range(2):
                for half in range(2):
                    b = proj_group(wb, g, half)
                    hs = slice(half * 512, (half + 1) * 512)
                    if kind == "z":
                        P.op("act", lambda e, b=b, c=ch + g, hs=hs: e.activation(out=gate.t[:, c, hs], in_=pf(b, 0, 512), func=AF.Silu),
                             reads=[PS[b]], writes=[gate])
                    else:
                        dstb = qsw if kind == "qsw" else qmem
                        P.op("dve", lambda e, b=b, c=ch + g, hs=hs, dstb=dstb: e.tensor_copy(out=dstb.t[:, c, hs], in_=pf(b, 0, 512)),
                             reads=[PS[b]], writes=[dstb])
            chk(4.3 + 0.01 * ti)
            if nxt < len(otiles):
                t = otiles[nxt]; nxt += 1
                pend.append(ring_load(wv(t[1], t[1] + 256)))

        chk(5)
        P.label = "mem"
        mring = [ring_load(w_mkv[:, i * 256:(i + 1) * 256].rearrange("(kc p) c -> p kc c", p=128)) for i in range(3)]
        P.label = "own"
        WQ = P.mk(None)
        uqv = w_uq[:, :].rearrange("(kc p) (h c) -> p kc h c", p=128, c=192)
        wuqv = wuq.t[:].rearrange("p kc (h c) -> p kc h c", c=256)
        for kc in range(4):
            P.dma("pool", wuqv[:, kc, :, 0:192], uqv[:, kc, :, :], WQ, writes=[wuq])
            P.dma("pool", wuqv[:, kc, :, 192:224], uqv[:, kc, :, 160:192], WQ, writes=[wuq])
            P.dma("pool", wuqv[:, kc, :, 224:256], uqv[:, kc, :, 128:160], WQ, writes=[wuq])
        P.dma("pool", wukv.t[:], w_ukv[:, :].rearrange("(kc p) c -> p kc c", p=128), WQ, writes=[wukv])
        wuq.w = P.totals(WQ)
        wukv.w = P.totals(WQ)

        P.label = "swa"
        bm = P.sb([128, 2, 8, 128], F32, R5)
        swm = P.sb([128, 2, 128], F32, R5 + 8 * KB)
        ssw = [P.sb([128, 512], F32, R5 + 9 * KB + 2 * KB * i) for i in range(2)]
        psw = [P.sb([128, 512], BF16, R5 + 13 * KB + KB * i) for i in range(6)]
        sinkf = P.sb([1, 1024], F32, 187 * KB)
        sinkh = P.sb([1, 1024], BF16, 191 * KB)
        sinkl = P.sb([1, 1024], BF16, 193 * KB)
        sinkt = P.sb([1, 1024], F32, 195 * KB)
        SWB = P.mk(None)
        P.dma("sp", sinkf.t[:], sink_d[:, :], SWB, writes=[sinkf])
        P.dma("sp", bm.t[:].rearrange("p a b c -> p (a b c)"), swab_d[:, :], SWB, writes=[bm])
        P.dma("sp", swm.t[:].rearrange("p a b -> p (a b)"), swam_d[:, :], SWB, writes=[swm])
        bm.w = P.totals(SWB)
        swm.w = P.totals(SWB)
        sinkf.w = P.totals(SWB)
        P.label = "mem"
        kmT = P.sb([128, 4, 256], BF16, R7)
        vmem = P.sb([128, 2, 512], BF16, R7 + 2 * KB)
        pm = [P.sb([128, 2, 512], BF16, R7 + 4 * KB + 2 * KB * i) for i in range(2)]
        rinv = [P.sb([128, 512], F32, R7 + 8 * KB + 2 * KB * i) for i in range(2)]
        tmpo = P.sb([128, 512], F32, R7 + 12 * KB)
        for i in range(4):
            wb = mring[i] if i < 3 else ring_load(w_mkv[:, 768:1024].rearrange("(kc p) c -> p kc c", p=128))
            if i < 2:
                for g in range(2):
                    hd = i * 2 + g
                    b = obank[0] % 6
                    obank[0] += 1

                    def mm(e, wb=wb, g=g, b=b):
                        ins = None
                        for kc in range(16):
                            ins = e.matmul(pf(b, 0, 256), lhsT=wb.t[:, kc, g * 128:(g + 1) * 128], rhs=memT.t[:, kc, :],
                                           start=(kc == 0), stop=(kc == 15))
                        return ins
                    P.op("pe", mm, reads=[wb, memT, memT_hi], writes=[PS[b]])
                    P.op("dve", lambda e, b=b, hd=hd: e.tensor_copy(out=kmT.t[:, hd, :], in_=pf(b, 0, 256)), reads=[PS[b]], writes=[kmT])
            else:
                b = obank[0] % 6
                obank[0] += 1

                def mm(e, wb=wb, b=b):
                    ins = None
                    for mt in range(2):
                        for kc in range(16):
                            ins = e.matmul(pf(b, mt * 256, (mt + 1) * 256), lhsT=memT.t[:, kc, mt * 128:(mt + 1) * 128],
                                           rhs=wb.t[:, kc, :], start=(kc == 0), stop=(kc == 15))
                    return ins
                P.op("pe", mm, reads=[wb, memT, memT_hi], writes=[PS[b]])
                cc = (i - 2) * 256
                P.op("dve", lambda e, b=b, cc=cc: e.tensor_copy(out=vmem.t[:, :, cc:cc + 256],
                                                                in_=pf(b, 0, 512).rearrange("p (a b) -> p a b", b=256)),
                     reads=[PS[b]], writes=[vmem])

        def finish(obk, lbk, ncol, pi, mixdst_fn):
            rv = rinv[pi % 2]
            P.op("act", lambda e: e.activation(out=rv.t[:, 0:ncol], in_=pf(lbk, 0, ncol), func=AF.Ln), reads=[PS[lbk]], writes=[rv])
            P.op("act", lambda e: e.activation(out=rv.t[:, 0:ncol], in_=rv.t[:, 0:ncol], func=AF.Exp, scale=-1.0), reads=[rv], writes=[rv])
            P.op("dve", lambda e: e.tensor_tensor(out=tmpo.t[:, 0:ncol], in0=pf(obk, 0, ncol), in1=rv.t[:, 0:ncol], op=ALU.mult),
                 reads=[PS[obk], rv], writes=[tmpo])
            mixdst_fn()

        def memA(it):
            hd, half = it // 2, it % 2
            hs = slice(half * 512, (half + 1) * 512)
            sb_ = [(it % 2) * 2, (it % 2) * 2 + 1]
            pmb = pm[it % 2]

            def mms(e):
                ins = None
                for mt in range(2):
                    ins = e.matmul(pf(sb_[mt], 0, 512), lhsT=kmT.t[:, hd, mt * 128:(mt + 1) * 128], rhs=qmem.t[:, hd, hs],
                                   start=True, stop=True)
                return ins
            P.op("pe", mms, reads=[kmT, qmem], writes=[PS[sb_[0]], PS[sb_[1]]])
            for mt in range(2):
                P.op("act", lambda e, mt=mt: e.activation(out=pmb.t[:, mt, :], in_=pf(sb_[mt], 0, 512),
                                                          func=AF.Exp, scale=MEM_SCALE),
                     reads=[PS[sb_[mt]]], writes=[pmb])

        def memB(it):
            hd, half = it // 2, it % 2
            hs = slice(half * 512, (half + 1) * 512)
            ob, lb = 4 + (it % 2), 6 + (it % 2)
            pmb = pm[it % 2]

            def mmo(e):
                for mt in range(2):
                    e.matmul(pf(ob, 0, 512), lhsT=vmem.t[:, mt, hd * 128:(hd + 1) * 128], rhs=pmb.t[:, mt, :],
                             start=(mt == 0), stop=(mt == 1))
                ins = None
                for mt in range(2):
                    ins = e.matmul(pf(lb, 0, 512), lhsT=ones.t[:], rhs=pmb.t[:, mt, :], start=(mt == 0), stop=(mt == 1))
                return ins
            P.op("pe", mmo, reads=[vmem, pmb, ones], writes=[PS[ob], PS[lb]])

            def mixm():
                P.op("pool", lambda e: e.tensor_tensor(out=gate.t[:, 12 + hd, hs], in0=tmpo.t[:], in1=gate.t[:, 12 + hd, hs], op=ALU.mult),
                     reads=[tmpo, gate], writes=[gate])
            finish(ob, lb, 512, it, mixm)

        for k in range(9):
            if k < 8:
                memA(k)
            if k >= 1:
                memB(k - 1)

        chk(6)
        P.label = "swa"
        for kt in range(2):
            P.op("pool", lambda e, kt=kt: e.tensor_tensor(out=bm.t[:, kt, :, :], in0=bm.t[:, kt, :, :],
                                                          in1=swm.t[:, kt:kt + 1, :].to_broadcast([128, 8, 128]), op=ALU.add),
                 reads=[bm, swm], writes=[bm])
        P.op("act", lambda e: e.activation(out=sinkt.t[:], in_=sinkf.t[:], func=AF.Exp), reads=[sinkf], writes=[sinkt])
        P.op("dve", lambda e: e.tensor_copy(out=sinkh.t[:], in_=sinkt.t[:]), reads=[sinkt], writes=[sinkh])
        P.op("dve", lambda e: e.tensor_copy(out=sinkf.t[:], in_=sinkh.t[:]), reads=[sinkh], writes=[sinkf])
        P.op("dve", lambda e: e.tensor_tensor(out=sinkl.t[:], in0=sinkt.t[:], in1=sinkf.t[:], op=ALU.subtract),
             reads=[sinkt, sinkf], writes=[sinkl])

        def swinfo(it):
            qb, kvh = it // 2, it % 2
            ui = qb // 2
            tiles = [(8 + ui) if qb % 2 == 0 else (qb - 1), qb]
            return qb, kvh, tiles, (kk0 if kvh == 0 else kk1)

        def swaA(it, part):
            qb, kvh, tiles, kk = swinfo(it)

            def mms(e):
                ins = None
                for kt in range(2):
                    for hq in range(4):
                        hd = kvh * 4 + hq
                        par, hq2 = hq % 2, hq // 2
                        g, pr = hd // 2, par * 64
                        ins = e.matmul(pf(kt * 2 + par, hq2 * 128, (hq2 + 1) * 128),
                                       lhsT=kk.t[pr:pr + 64, tiles[kt] * 128:(tiles[kt] + 1) * 128],
                                       rhs=qsw.t[pr:pr + 64, g, qb * 128:(qb + 1) * 128], start=True, stop=True)
                return ins
            if part == 1:
                P.op("pe", mms, reads=[kk, qsw], writes=[PS[0], PS[1], PS[2], PS[3]])
            for kt in range(2):
                sw_ = ssw[kt]
                pb_ = psw[(it % 3) * 2 + kt]
                if part == 1:
                    P.op("dve", lambda e, kt=kt, sw_=sw_: e.scalar_tensor_tensor(
                        out=sw_.t[:].rearrange("p (a b) -> p a b", a=2),
                        in0=DB[kt][:, :].rearrange("p (b c) -> p b c", b=2)[:, :, 0:256], scalar=0.125,
                        in1=bm.t[:, kt, kvh * 4:(kvh + 1) * 4, :].rearrange("p (a b) c -> p a (b c)", a=2), op0=ALU.mult, op1=ALU.add),
                        reads=[PS[kt * 2], PS[kt * 2 + 1], bm], writes=[sw_])
                    continue
                if kt == 0:
                    P.op("act", lambda e, sw_=sw_, pb_=pb_: e.activation(out=pb_.t[:], in_=sw_.t[:], func=AF.Exp,
                                                                          bias=kvb.t[:, qb:qb + 1], scale=1.0),
                         reads=[sw_, kvb], writes=[pb_])
                else:
                    P.op("act", lambda e, sw_=sw_, pb_=pb_: e.activation(out=pb_.t[:], in_=sw_.t[:], func=AF.Exp),
                         reads=[sw_], writes=[pb_])

        def swaB(it):
            qb, kvh, tiles, kk = swinfo(it)
            ob, lb = 4 + (it % 2), 6 + (it % 2)
            pbs = [psw[(it % 3) * 2 + kt] for kt in range(2)]

            def mmo(e):
                for kt in range(2):
                    e.matmul(pf(ob, 0, 512), lhsT=vv.t[:, tiles[kt], kvh * 128:(kvh + 1) * 128], rhs=pbs[kt].t[:],
                             start=(kt == 0), stop=(kt == 1))
                for kt in range(2):
                    e.matmul(pf(lb, 0, 512), lhsT=ones.t[:], rhs=pbs[kt].t[:], start=(kt == 0), stop=False)
                e.matmul(pf(lb, 0, 512), lhsT=ones.t[0:1, :], rhs=sinkh.t[0:1, kvh * 512:(kvh + 1) * 512], start=False, stop=False)
                return e.matmul(pf(lb, 0, 512), lhsT=ones.t[0:1, :], rhs=sinkl.t[0:1, kvh * 512:(kvh + 1) * 512], start=False, stop=True)
            P.op("pe", mmo, reads=[vv, ones, sinkh, sinkl] + pbs, writes=[PS[ob], PS[lb]])

            def mixs():
                for par in range(2):
                    pr = par * 64
                    c8 = 8 + kvh * 2
                    P.op("pool", lambda e, par=par, pr=pr, c8=c8: e.tensor_tensor(
                        out=gate.t[pr:pr + 64, c8:c8 + 2, qb * 128:(qb + 1) * 128],
                        in0=tmpo.t[pr:pr + 64, par * 256:(par + 1) * 256].rearrange("p (a b) -> p a b", a=2),
                        in1=gate.t[pr:pr + 64, c8:c8 + 2, qb * 128:(qb + 1) * 128], op=ALU.mult), reads=[tmpo, gate], writes=[gate])
            finish(ob, lb, 512, it, mixs)

        for k in range(18):
            if k < 16:
                swaA(k, 1)
            if k >= 2:
                swaB(k - 2)
            if k < 16:
                swaA(k, 2)

        chk(7)
        P.label = "mla"
        Kb = [P.sb([128, S], BF16, R5 + 8 * KB * i) for i in range(2)]
        Vb = [P.sb([128, 32, 128], BF16, R5 + 16 * KB + 8 * KB * i) for i in range(2)]
        Qn = [P.sb([128, 1024], BF16, R6 + 2 * KB * i) for i in range(2)]
        Qp = [P.sb([128, 1024], BF16, R6 + 4 * KB + 2 * KB * i) for i in range(2)]
        csq = P.sb([128, 1024], F32, R6 + 8 * KB)
        tq = P.sb([128, 512], F32, R6 + 12 * KB)
        tq2 = P.sb([128, 512], F32, R6 + 14 * KB)
        pmla = [P.sb([128, 512], BF16, R6 + 16 * KB + KB * i) for i in range(3)]
        p2s = [P.sb([128, 256], BF16, R6 + 22 * KB + 512 * i) for i in range(3)]
        rinvm = [P.sb([128, 256], F32, R6 + 19 * KB + KB * i) for i in range(2)]
        tmpm = P.sb([128, 256], F32, R6 + 21 * KB)
        QC = P.mk(None)
        P.dma("sp", csq.t[:], csk[:, 3072:4096], QC, writes=[csq])
        for i in range(2):
            P.dma("pool", Qp[i].t[64:128, :], gtm[:, :], QC, writes=[Qp[i]])
        for b_ in (csq, Qp[0], Qp[1]):
            b_.w = b_.w + P.totals(QC)

        GEN_BANK = 7

        def gen_chunks(h, banks=(7,)):
            s = h % 2
            K_, V_, Qn_, Qp_ = Kb[s], Vb[s], Qn[s], Qp[s]
            ch = []
            for half in range(2):
                gb = banks[len(ch) % len(banks)]
                def qn(half=half, gb=gb):
                    def mm(e):
                        ins = None
                        for kc in range(4):
                            ins = e.matmul(pf(gb, 0, 512), lhsT=wuq.t[:, kc, h * 256:h * 256 + 128],
                                           rhs=cqT.t[:, kc, half * 512:(half + 1) * 512], start=(kc == 0), stop=(kc == 3))
                        return ins
                    P.op("pe", mm, reads=[wuq, cqT], writes=[PS[gb]])
                    P.op("dve", lambda e: e.tensor_copy(out=Qn_.t[:, half * 512:(half + 1) * 512], in_=pf(gb, 0, 512)),
                         reads=[PS[gb]], writes=[Qn_])
                ch.append(qn)

                gb = banks[len(ch) % len(banks)]
                def qp(half=half, gb=gb):
                    def mm(e):
                        ins = None
                        for kc in range(4):
                            ins = e.matmul(pf(gb, 0, 512), lhsT=wuq.t[:, kc, h * 256 + 128:h * 256 + 256],
                                           rhs=cqT.t[:, kc, half * 512:(half + 1) * 512], start=(kc == 0), stop=(kc == 3))
                        return ins
                    P.op("pe", mm, reads=[wuq, cqT], writes=[PS[gb]])
                    P.op("dve", lambda e: e.tensor_tensor(out=tq.t[:], in0=pf(gb, 0, 512), in1=csq.t[:, half * 512:(half + 1) * 512], op=ALU.mult),
                         reads=[PS[gb], csq], writes=[tq])
                    P.op("pool", lambda e: e.tensor_copy(out=tq2.t[0:64, :], in_=tq.t[64:128, :]), reads=[tq], writes=[tq2])
                    P.op("pool", lambda e: e.tensor_tensor(out=Qp_.t[0:64, half * 512:(half + 1) * 512], in0=tq.t[0:64, :], in1=tq2.t[0:64, :], op=ALU.add),
                         reads=[tq, tq2], writes=[Qp_])
                ch.append(qp)
            for c8 in range(8):
                gb = banks[len(ch) % len(banks)]
                def kg(c8=c8, gb=gb):
                    def mm(e):
                        ins = None
                        for kc in range(2):
                            ins = e.matmul(pf(gb, 0, 512), lhsT=wukv.t[:, kc, h * 256:h * 256 + 128],
                                           rhs=kvnT.t[:, kc, c8 * 512:(c8 + 1) * 512], start=(kc == 0), stop=(kc == 1))
                        return ins
                    P.op("pe", mm, reads=[wukv, kvnT], writes=[PS[gb]])
                    P.op("dve", lambda e: e.tensor_copy(out=K_.t[:, c8 * 512:(c8 + 1) * 512], in_=pf(gb, 0, 512)),
                         reads=[PS[gb]], writes=[K_])
                ch.append(kg)

                gb = banks[len(ch) % len(banks)]
                def vg(c8=c8, gb=gb):
                    def mm(e):
                        ins = None
                        for t4 in range(4):
                            tl = c8 * 4 + t4
                            for kc in range(2):
                                ins = e.matmul(pf(gb, t4 * 128, (t4 + 1) * 128), lhsT=kvnT.t[:, kc, tl * 128:(tl + 1) * 128],
                                               rhs=wukv.t[:, kc, h * 256 + 128:h * 256 + 256], start=(kc == 0), stop=(kc == 1))
                        return ins
                    P.op("pe", mm, reads=[wukv, kvnT], writes=[PS[gb]])
                    P.op("dve", lambda e: e.tensor_copy(out=V_.t[:, c8 * 4:(c8 + 1) * 4, :], in_=pf(gb, 0, 512).rearrange("p (a b) -> p a b", b=128)),
                         reads=[PS[gb]], writes=[V_])
                ch.append(vg)
            return ch

        for c in gen_chunks(0, banks=(0, 1, 2, 3, 4, 5, 6, 7)):
            c()

        kpeT.w = kpeT.w + conb_tot
        P.label = "outpre"
        wo_t = [ring_load(w_out[:, i * 256:(i + 1) * 256].rearrange("(kc p) c -> p kc c", p=128)) for i in range(3)]
        P.label = "mla"
        steps = []
        for h in range(8):
            for i in range(4):
                kus = list(range(3 * (i + 1))) + list(range(12, 13 + i))
                for n, ku in enumerate(kus):
                    steps.append((h, i, ku, n == 0, n == len(kus) - 1))
        SKEW = 2
        pendq = []
        sidx = [0]
        gen_next = []
        cur_h = -1

        def rec_pv(item):
            (h, i, ku, first, last, pbuf, p2) = item
            s = h % 2
            ob, lb = 3 + (i % 2), 5 + (i % 2)

            def mm(e):
                for t in range(2):
                    tl = ku * 2 + t
                    e.matmul(pf(ob, 0, 256), lhsT=Vb[s].t[:, tl, :], rhs=pbuf.t[:, t * 256:(t + 1) * 256],
                             start=(first and t == 0), stop=(last and t == 1))
                return e.matmul(pf(lb, 0, 256), lhsT=ones.t[:], rhs=p2.t[:], start=first, stop=last)
            P.op("pe", mm, reads=[Vb[s], pbuf, p2, ones], writes=[PS[ob], PS[lb]])
            if last:
                rv = rinvm[i % 2]
                P.op("act", lambda e: e.activation(out=rv.t[:], in_=pf(lb, 0, 256), func=AF.Ln), reads=[PS[lb]], writes=[rv])
                P.op("act", lambda e: e.activation(out=rv.t[:], in_=rv.t[:], func=AF.Exp, scale=-1.0), reads=[rv], writes=[rv])
                P.op("dve", lambda e: e.tensor_tensor(out=tmpm.t[:], in0=pf(ob, 0, 256), in1=rv.t[:], op=ALU.mult),
                     reads=[PS[ob], rv], writes=[tmpm])
                P.op("pool", lambda e: e.tensor_tensor(out=gate.t[:, h, i * 256:(i + 1) * 256], in0=tmpm.t[:],
                                                       in1=gate.t[:, h, i * 256:(i + 1) * 256], op=ALU.mult),
                     reads=[tmpm, gate], writes=[gate])

        for (h, i, ku, first, last) in steps:
            if h != cur_h:
                assert len(pendq) <= SKEW
                cur_h = h
                gen_next = gen_chunks(h + 1) if h + 1 < 8 else []
            s = h % 2
            sbank = sidx[0] % 3
            pbuf = pmla[sidx[0] % 3]
            sidx[0] += 1

            def mms(e, s=s, i=i, ku=ku, sbank=sbank):
                ins = None
                for t in range(2):
                    kc0 = ku * 256 + t * 128
                    e.matmul(pf(sbank, t * 256, (t + 1) * 256), lhsT=Kb[s].t[:, kc0:kc0 + 128], rhs=Qn[s].t[:, i * 256:(i + 1) * 256],
                             start=True, stop=False)
                    ins = e.matmul(pf(sbank, t * 256, (t + 1) * 256), lhsT=kpeT.t[:, kc0:kc0 + 128], rhs=Qp[s].t[:, i * 256:(i + 1) * 256],
                                   start=False, stop=True)
                return ins
            P.op("pe", mms, reads=[Kb[s], Qn[s], Qp[s], kpeT], writes=[PS[sbank]])
            P.op("act", lambda e, sbank=sbank, pbuf=pbuf: e.activation(out=pbuf.t[:], in_=pf(sbank, 0, 512), func=AF.Exp, scale=MLA_SCALE),
                 reads=[PS[sbank]], writes=[pbuf])
            p2 = p2s[(sidx[0] - 1) % 3]
            P.op("dve", lambda e, pbuf=pbuf, p2=p2: e.tensor_tensor(out=p2.t[:], in0=pbuf.t[:, 0:256], in1=pbuf.t[:, 256:512], op=ALU.add),
                 reads=[pbuf], writes=[p2])
            pendq.append((h, i, ku, first, last, pbuf, p2))
            if len(pendq) > SKEW:
                rec_pv(pendq.pop(0))
            if gen_next and (sidx[0] % 2 == 0):
                gen_next.pop(0)()
        while pendq:
            rec_pv(pendq.pop(0))
        assert not gen_next

        chk(8)
        P.label = "out"
        res = P.sb([128, 8, D], F32, R5)
        gfin = P.sb([128, D], F32, 177 * KB)
        xo = [P.sb([128, D], F32, R1 + 32 * KB + 8 * KB * i) for i in range(4)]
        GF = P.mk(None)
        P.dma("sp", gfin.t[:], gfin_d[:, :], GF, writes=[gfin])
        for tt in range(4):
            P.dma("sp", xo[tt].t[:], xp[3072 + tt * 128:3072 + (tt + 1) * 128, :], xo[tt], writes=[xo[tt]])
        obk = [0]

        def F1(tt):
            x_ = xo[tt % 7]
            sv = ssv[tt % 4]
            P.op("dve", lambda e: e.tensor_tensor(out=x_.t[:], in0=res.t[:, tt, :], in1=x_.t[:], op=ALU.add),
                 reads=[res, x_], writes=[x_])
            P.op("act", lambda e: e.activation(out=res.t[:, tt, :], in_=x_.t[:], func=AF.Square, accum_out=sv.t[:, 0:1]),
                 reads=[x_], writes=[res, sv])
            P.op("act", lambda e: e.activation(out=sv.t[:, 1:2], in_=sv.t[:, 0:1], func=AF.Ln, bias=EPS, scale=1.0 / D),
                 reads=[sv], writes=[sv])
            P.op("act", lambda e: e.activation(out=sv.t[:, 2:3], in_=sv.t[:, 1:2], func=AF.Exp, scale=-0.5),
                 reads=[sv], writes=[sv])

        def F2(tt):
            x_ = xo[tt % 7]
            sv = ssv[tt % 4]
            P.op("dve", lambda e: e.scalar_tensor_tensor(out=x_.t[:], in0=x_.t[:], scalar=sv.t[:, 2:3], in1=gfin.t[:],
                                                         op0=ALU.mult, op1=ALU.mult),
                 reads=[x_, sv, gfin], writes=[x_])
            P.dma("sp", out_d[tt * 128:(tt + 1) * 128, :], x_.t[:], OUTS[tt % 2], reads=[x_])
            if tt + 7 < 8:
                P.dma("sp", x_.t[:], xp[3072 + (tt + 7) * 128:3072 + (tt + 8) * 128, :], x_, writes=[x_])

        def fin_items(t0, n=4):
            items = []
            for k in range(n + 1):
                it = []
                if k < n:
                    it.append((F1, t0 + k))
                if k >= 1:
                    it.append((F2, t0 + k - 1))
                items.append(it)
            return items

        nload = [3]
        for half in range(1):
            pend_fin = []
            for ci in range(8):
                P.label = "out"
                wb = wo_t.pop(0)
                for tt in range(8):
                    b = obk[0] % 8
                    obk[0] += 1

                    def mm(e, wb=wb, tt=tt, b=b):
                        ins = None
                        for kc in range(16):
                            ins = e.matmul(pf(b, 0, 256), lhsT=gate.t[:, kc, tt * 128:(tt + 1) * 128], rhs=wb.t[:, kc, :],
                                           start=(kc == 0), stop=(kc == 15))
                        return ins
                    P.op("pe", mm, reads=[wb, gate], writes=[PS[b]])
                    if tt % 2 == 0:
                        P.op("act", lambda e, b=b, tt=tt, ci=ci: e.activation(out=res.t[:, tt, ci * 256:(ci + 1) * 256], in_=pf(b, 0, 256), func=AF.Copy),
                             reads=[PS[b]], writes=[res])
                    else:
                        P.op("dve", lambda e, b=b, tt=tt, ci=ci: e.tensor_copy(out=res.t[:, tt, ci * 256:(ci + 1) * 256], in_=pf(b, 0, 256)),
                             reads=[PS[b]], writes=[res])
                if nload[0] < 8:
                    cn = nload[0] % 8
                    nload[0] += 1
                    wo_t.append(ring_load(w_out[:, cn * 256:(cn + 1) * 256].rearrange("(kc p) c -> p kc c", p=128)))
                if pend_fin:
                    P.label = "fin"
                    for f, a in pend_fin.pop(0):
                        f(a)
        P.label = "fin"
        xo.extend([P.sb([128, D], F32, RING + 8 * KB * i) for i in range(3)])
        for tt in range(4, 7):
            P.dma("sp", xo[tt].t[:], xp[3072 + tt * 128:3072 + (tt + 1) * 128, :], xo[tt], writes=[xo[tt]])
        for it in fin_items(0, 8):
            for f, a in it:
                f(a)
    try:
        record()
    except _Stop:
        P.dma("sp", out_d[0:128, :], kvnT.t[:, 0, 0:1024].bitcast(F32).rearrange("p (a b) -> p a b", a=1)[:, 0, :] if False else kvnT.t[:].rearrange("p a b -> p (a b)").bitcast(F32)[:, 0:2048], OUTS[0], reads=[kvnT, kpeT])
    fin = []
    for ob_ in OUTS:
        fin += P.totals(ob_)

    blk = es.enter_context(nc.Block())

    @blk.sync
    def _(e):
        P.replay("sp", e)
        for k, v in fin:
            e.wait_ge(P.sem[k], v)

    @blk.gpsimd
    def _(e):
        P.replay("pool", e)

    @blk.scalar
    def _(e):
        P.replay("act", e)

    @blk.vector
    def _(e):
        P.replay("dve", e)

    @blk.tensor
    def _(e):
        P.replay("pe", e)

    es.close()
    nc._prog_labels = P.labels
    return nc


def _t5_bucket(rel):
    nb = 16
    max_exact = 8
    bucket = np.where(rel > 0, nb, 0)
    n = np.abs(rel)
    nf = np.maximum(n, 1).astype(np.float32)
    large = max_exact + (np.log(nf / max_exact) / np.log(128 / max_exact) * (nb - max_exact)).astype(np.int32)
    large = np.minimum(large, nb - 1)
    return bucket + np.where(n < max_exact, n, large)


def _own_units(j):
    return [j, 7 - j, 8 + j, 15 - j]


_NC_CACHE = {}


def _prepare(x, mem, norm_in, w_in, norm_q, norm_kv, w_uq, w_ukv, attn_sinks, rel_bias,
             norm_mem, w_mem_kv, w_out, norm_final):
    f32 = np.float32
    x = np.asarray(x, f32); mem = np.asarray(mem, f32)
    w_in0 = np.ascontiguousarray(np.asarray(w_in, f32)[0])
    w_uq0 = np.ascontiguousarray(np.asarray(w_uq, f32)[0])
    w_ukv0 = np.ascontiguousarray(np.asarray(w_ukv, f32)[0])
    w_mkv0 = np.ascontiguousarray(np.asarray(w_mem_kv, f32)[0])
    w_out0 = np.ascontiguousarray(np.asarray(w_out, f32)[0])
    bc = lambda v: np.ascontiguousarray(np.broadcast_to(np.asarray(v, f32).reshape(1, -1), (128, np.asarray(v).size)))
    gin = bc(norm_in[0]); gmem = bc(norm_mem[0]); gfin = bc(norm_final)
    gq = np.ascontiguousarray(np.asarray(norm_q, f32)[0].reshape(4, 128).T)
    gkv = np.ascontiguousarray(np.asarray(norm_kv, f32)[0].reshape(2, 128).T)
    inv = (1.0 / (np.float32(10000.0) ** (np.arange(0, 64, 2, dtype=f32) / np.float32(64)))).astype(f32)
    ang = (np.arange(S, dtype=f32)[:, None] * inv[None, :]).astype(f32)
    cos = np.cos(ang).astype(f32).T
    sin = np.sin(ang).astype(f32).T
    cs_nat = np.concatenate([cos, cos, -sin, sin], axis=0)
    qi = np.arange(128); kj = np.arange(256)
    rel = kj[None, :] - 128 - qi[:, None]
    bidx = _t5_bucket(rel)
    rb = np.asarray(rel_bias, f32)
    bias_qkh = rb[bidx]
    hperm = [kvh * 4 + hq2 * 2 + par for kvh in range(2) for par in range(2) for hq2 in range(2)]
    swab = np.ascontiguousarray(bias_qkh[:, :, hperm].transpose(1, 2, 0).reshape(2, 128, 8, 128).transpose(1, 0, 2, 3)).reshape(128, 2048)
    dq = (kj[None, :] // 64) - (qi[:, None] // 64)
    valid = (dq >= 0) & (dq <= 2)
    swam = np.where(valid, 0.0, -BIG).astype(f32)
    swam = np.ascontiguousarray(swam.T.reshape(2, 128, 128).transpose(1, 0, 2)).reshape(128, 256)
    sinkr = np.ascontiguousarray(np.repeat(np.asarray(attn_sinks, f32)[0][hperm], 128).reshape(1, 1024))
    ident = np.eye(128, dtype=f32)

    in_maps = []
    meta = []
    for core in range(8):
        b, j = core // 4, core % 4
        own = _own_units(j)
        others = [u for u in range(16) if u not in own]
        order = others + own
        tok = np.concatenate([np.arange(u * 256, (u + 1) * 256) for u in order])
        own_tok = tok[3072:]
        xb = x[b]
        pred = np.zeros((512, D), f32)
        kvbv = np.zeros((128, 8), f32)
        for i, u in enumerate(own):
            if u > 0:
                pred[i * 128:(i + 1) * 128] = xb[u * 256 - 128:u * 256]
            else:
                kvbv[:, 2 * i] = -BIG
        xpm = np.concatenate([xb[tok], pred], axis=0)
        chunk = tok // 64
        ohk = np.zeros((64, S), f32)
        ohk[chunk, np.arange(S)] = 1.0
        qchunk = own_tok // 64
        gt = np.where(np.arange(64)[:, None] > qchunk[None, :], -BIG, 0.0).astype(f32)
        in_maps.append({
            "xp": np.ascontiguousarray(xpm), "memx": np.ascontiguousarray(mem[b]),
            "csk": np.ascontiguousarray(cs_nat[:, tok]), "ohk": ohk, "gtm": gt,
            "w_in": w_in0, "w_uq": w_uq0, "w_ukv": w_ukv0, "w_mkv": w_mkv0, "w_out": w_out0,
            "gin": gin, "gmem": gmem, "gfin": gfin, "gq": gq, "gkv": gkv,
            "swab": swab, "swam": swam, "kvb": kvbv, "sinkr": sinkr, "ident": ident,
        })
        meta.append((b, own_tok))
    return in_maps, meta


def kernel(x, mem, norm_in, w_in, norm_q, norm_kv, w_uq, w_ukv, attn_sinks, rel_bias,
           norm_mem, w_mem_kv, w_out, norm_final):
    in_maps, meta = _prepare(x, mem, norm_in, w_in, norm_q, norm_kv, w_uq, w_ukv, attn_sinks, rel_bias,
                             norm_mem, w_mem_kv, w_out, norm_final)
    f32 = np.float32
    if "nc" not in _NC_CACHE:
        _NC_CACHE["nc"] = build()
    res = run_bass_kernel_spmd(_NC_CACHE["nc"], in_maps, core_ids=list(range(8)))
    out = np.zeros((NB, S, D), f32)
    for core in range(8):
        b, own_tok = meta[core]
        out[b, own_tok] = res.results[core]["out"]
    return out
```

```python
import contextlib
import numpy as np
import concourse.bass as bass
import concourse.mybir as mybir
from concourse.bass_utils import run_bass_kernel_spmd

F32 = mybir.dt.float32
BF16 = mybir.dt.bfloat16
AF = mybir.ActivationFunctionType
ALU = mybir.AluOpType

D = 2048
S = 4096
NB = 2
KB = 1024
SB0 = 16512
EPS = 1e-6
BIG = 30000.0
MLA_SCALE = 192.0 ** -0.5
MEM_SCALE = 128.0 ** -0.5

C_CQ, C_CKV, C_KPE, C_ZMLA, C_QSWA, C_KSWA, C_VSWA, C_ZSWA, C_QMEM, C_ZMEM = (
    0, 512, 768, 832, 1856, 2368, 2496, 2624, 3136, 3648)


class Buf:
    __slots__ = ("t", "w", "r", "dkey", "dcnt", "ps")


class Prog:
    ENG = ("pe", "act", "dve", "pool", "sp")
    CE = ("pe", "act", "dve", "pool")

    def __init__(self, nc, es):
        self.nc = nc
        self.es = es
        self.ops = {e: [] for e in self.ENG}
        self.cnt = {e: 0 for e in self.CE}
        self.known = {e: {} for e in self.ENG}
        self.sem = {}
        for e in self.CE:
            self.sem[e] = es.enter_context(nc.semaphore("sem_" + e))
        self.nd = 0
        self.nt = 0
        self.label = ""
        self.labels = {e: [] for e in self.ENG}
        self.oplabels = {e: [] for e in self.ENG}

    def sb(self, shape, dtype, off, fresh=False):
        self.nt += 1
        t = self.nc.alloc_sbuf_tensor_at("sb%d" % self.nt, list(shape), dtype, offset=SB0 + off)
        return self.mk(t, fresh)

    def mk(self, t, fresh=True):
        b = Buf()
        b.t = t
        b.w = [] if fresh else [(e, c) for e, c in self.cnt.items() if c > 0]
        b.r = []
        b.dkey = None
        b.dcnt = 0
        b.ps = False
        return b

    def _waits(self, eng, reads, writes, skip=None):
        need = {}

        def add(ev):
            k, v = ev
            if k == skip:
                return
            if v > need.get(k, 0):
                need[k] = v
        for b in reads:
            for ev in b.w:
                add(ev)
        for b in writes:
            for ev in b.w:
                add(ev)
            for ev in b.r:
                add(ev)
        kn = self.known[eng]
        wl = []
        for k, v in need.items():
            if eng == "pe" and k == "pe":
                continue
            if kn.get(k, 0) >= v:
                continue
            kn[k] = v
            wl.append((k, v))
        return wl

    def op(self, eng, fn, reads=(), writes=()):
        psr = [b for b in reads if b.ps]
        if psr:
            reads = [b for b in reads if not b.ps]
            writes = list(writes) + [b for b in psr if b not in writes]
        wl = self._waits(eng, reads, writes)
        self.cnt[eng] += 1
        ev = (eng, self.cnt[eng])
        self.ops[eng].append((wl, fn, eng))
        self.oplabels[eng].append(self.label)
        for b in reads:
            b.r.append(ev)
        for b in writes:
            b.w = [ev]
            b.r = []
        return ev

    def dma(self, q, out_ap, in_ap, owner, reads=(), writes=()):
        if owner.dkey is None:
            owner.dkey = {}
            owner.dcnt = {}
        if q not in owner.dkey:
            self.nd += 1
            owner.dkey[q] = ("d", self.nd)
            owner.dcnt[q] = 0
            self.sem[owner.dkey[q]] = self.es.enter_context(self.nc.semaphore("dsem%d" % self.nd))
        key = owner.dkey[q]
        wl = self._waits(q, reads, writes, skip=key)
        owner.dcnt[q] += 16
        ev = (key, owner.dcnt[q])
        self.ops[q].append((wl, lambda e: e.dma_start(out=out_ap, in_=in_ap), key))
        self.oplabels[q].append(self.label + "/dma")
        for b in reads:
            b.r.append(ev)
        for b in writes:
            b.w = [ev]
            b.r = []
        return ev

    def totals(self, owner):
        return [(owner.dkey[q], owner.dcnt[q]) for q in (owner.dkey or {})]

    def replay(self, eng, handle):
        class _Cnt:
            def __init__(s_, h):
                s_.h = h
                s_.n = 0

            def __getattr__(s_, name):
                a = getattr(s_.h, name)
                if callable(a):
                    def w(*args, **kw):
                        s_.n += 1
                        return a(*args, **kw)
                    return w
                return a
        for (wl, fn, sig), lab in zip(self.ops[eng], self.oplabels[eng]):
            for k, v in wl:
                handle.wait_ge(self.sem[k], v)
            c = _Cnt(handle)
            ins = fn(c)
            self.labels[eng].append((lab, c.n, [(str(k), v) for k, v in wl]))
            if sig is None:
                continue
            if isinstance(sig, tuple):
                ins.then_inc(self.sem[sig], 16)
            else:
                ins.then_inc(self.sem[sig], 1)


class _Stop(Exception):
    pass


def build(stop=99):
    nc = bass.Bass("TRN2", target_bir_lowering=False)
    es = contextlib.ExitStack()
    P = Prog(nc, es)

    def din(name, shape):
        return nc.dram_tensor(name, list(shape), F32, kind="ExternalInput").ap()

    xp = din("xp", [S + 512, D])
    memx = din("memx", [256, D])
    csk = din("csk", [128, S])
    ohk = din("ohk", [64, S])
    gtm = din("gtm", [64, 1024])
    w_in = din("w_in", [D, 4160])
    w_uq = din("w_uq", [512, 1536])
    w_ukv = din("w_ukv", [256, 2048])
    w_mkv = din("w_mkv", [D, 1024])
    w_out = din("w_out", [D, D])
    gin_d = din("gin", [128, D])
    gmem_d = din("gmem", [128, D])
    gfin_d = din("gfin", [128, D])
    gq_d = din("gq", [128, 4])
    gkv_d = din("gkv", [128, 2])
    swab_d = din("swab", [128, 2 * 8 * 128])
    swam_d = din("swam", [128, 2 * 128])
    kvb_d = din("kvb", [128, 8])
    sink_d = din("sinkr", [1, 1024])
    ident_d = din("ident", [128, 128])
    out_d = nc.dram_tensor("out", [1024, D], F32, kind="ExternalOutput").ap()

    DB = [es.enter_context(nc.psum_tensor("psd%d" % i, [128, 1024], F32)) for i in range(4)]
    PS = [P.mk(None) for _ in range(8)]
    for b_ in PS:
        b_.ps = True

    def pf(b, c0, c1, p0=0, p1=128):
        o = (b % 2) * 512
        return DB[b // 2][p0:p1, o + c0:o + c1]

    def pb2(k, c0, c1):
        return DB[k][:].bitcast(BF16)[:, c0:c1]

    def pb(b, c0, c1):
        o = (b % 2) * 1024
        return DB[b // 2][:].bitcast(BF16)[:, o + c0:o + c1]

    o = 0
    ident = P.sb([128, 128], BF16, o, True); o += 256
    ones = P.sb([128, 128], BF16, o, True); o += 256
    gq = P.sb([128, 4], F32, o, True); o += 32
    gkv = P.sb([128, 2], F32, o, True); o += 32
    kvb = P.sb([128, 8], F32, o, True); o += 32
    ssv = [P.sb([128, 4], F32, o + 32 * i, True) for i in range(4)]; o += 128
    assert o <= 1 * KB
    R0 = 1 * KB
    kvnT = P.sb([128, 2, S], BF16, R0, True)
    kpeT = P.sb([128, S], BF16, R0 + 16 * KB, True)
    R1 = R0 + 24 * KB
    RING = R1 + 64 * KB
    ring = [P.sb([128, 16, 256], BF16, RING + 8 * KB * i, True) for i in range(3)]
    R5 = RING + 24 * KB
    hT_own = P.sb([128, 16, 1024], BF16, R5, True)
    hT_own_hi = P.mk(hT_own.t)
    R6 = R5 + 32 * KB
    kk0 = P.sb([128, 12 * 128], BF16, R6, True)
    kk1 = P.sb([128, 12 * 128], BF16, R6 + 3 * KB, True)
    vv = P.sb([128, 12, 256], BF16, R6 + 6 * KB, True)
    R6b = R6 + 12 * KB
    R7 = R6b + 16 * KB
    memT = P.sb([128, 16, 256], BF16, 199 * KB, True)
    memT_hi = P.mk(memT.t)

    wkv = P.sb([128, 16, 640], BF16, R1, True)
    gin = P.sb([128, D], F32, R1 + 20 * KB, True)
    xt = [P.sb([128, D], F32, R1 + 28 * KB + 8 * KB * i, True) for i in range(4)]
    hh = [P.sb([128, D], BF16, R1 + 60 * KB, True), P.sb([128, D], BF16, R6b, True),
          P.sb([128, D], BF16, R6b + 4 * KB, True)]
    hT_rot = [P.sb([128, 16, 256], BF16, R6b + 8 * KB, True), P.sb([128, 16, 256], BF16, R6b + 16 * KB, True)]
    hT_rot_hi = [P.mk(hT_rot[0].t), P.mk(hT_rot[1].t)]
    HI = {id(hT_rot[0]): hT_rot_hi[0], id(hT_rot[1]): hT_rot_hi[1], id(hT_own): hT_own_hi, id(memT): memT_hi}
    a_sc = R6b + 24 * KB
    sq = [P.sb([128, 2, 256], BF16, a_sc + KB * i, True) for i in range(2)]
    rstd2 = [P.sb([128, 256], F32, a_sc + 2 * KB + KB * i, True) for i in range(2)]
    tt_ = [P.sb([128, 256], F32, a_sc + 4 * KB, True)] * 2
    t2_ = P.sb([128, 256], F32, a_sc + 6 * KB, True)
    cst = [P.sb([128, 256], F32, a_sc + 7 * KB + KB * i, True) for i in range(2)]
    vvTs = [P.sb([128, 2, 256], BF16, a_sc + 9 * KB, True), P.sb([128, 2, 256], BF16, a_sc + 5 * KB, True)]
    gmem = P.sb([128, D], F32, 191 * KB, True)

    OUTS = [P.mk(None), P.mk(None)]

    def chk(n):
        if stop <= n:
            raise _Stop()

    def record():
        def wv(c0, c1):
            return w_in[:, c0:c1].rearrange("(kc p) c -> p kc c", p=128)
        CONA = P.mk(None)
        wkv_sw = P.mk(wkv.t)
        P.dma("pool", ident.t[:], ident_d[:, :], CONA)
        P.dma("sp", gin.t[:], gin_d[:, :], CONA)
        P.dma("sp", gkv.t[:], gkv_d[:, :], CONA)
        for b in (ident, gin, gkv):
            b.w = P.totals(CONA)
        CONA2 = P.mk(None)
        for (dst, src, n) in [(0, C_CKV, 256), (256, C_KPE, 64), (320, C_KPE + 32, 32), (352, C_KPE, 32)]:
            P.dma("pool", wkv.t[:, :, dst:dst + n], wv(src, src + n), CONA2)
        wkv.w = P.totals(CONA2)

        conb_tot = []

        def load_conb():
            CONB = P.mk(None)
            P.dma("sp", gq.t[:], gq_d[:, :], CONB)
            P.dma("sp", kvb.t[:], kvb_d[:, :], CONB)
            P.dma("sp", gmem.t[:], gmem_d[:, :], CONB)
            for (dst, src, n) in [(384, C_KSWA, 128), (512, C_VSWA, 128)]:
                P.dma("pool", wkv.t[:, :, dst:dst + n], wv(src, src + n), CONB)
            P.dma("pool", kpeT.t[64:128, :], ohk[:, :], CONB)
            for b in (gq, kvb, gmem, wkv_sw):
                b.w = P.totals(CONB)
            kpeT.w = kpeT.w + P.totals(CONB)
            conb_tot.extend(P.totals(CONB))
        P.op("pool", lambda e: e.memset(ones.t[:], 1.0), writes=[ones])
        chk(1)

        ring_i = [0]

        def ring_load(src_ap):
            b = ring[ring_i[0] % 3]
            ring_i[0] += 1
            P.dma("pool", b.t[:], src_ap, b, writes=[b])
            return b

        NT = 38

        def tinfo(t):
            u, tl = t // 2, t % 2
            if u < 12:
                hTb, c0 = hT_rot[u % 2], 0
            elif u < 16:
                hTb, c0 = hT_own, (u - 12) * 256
            elif u < 18:
                hTb, c0 = hT_rot[u % 2], 0
            else:
                hTb, c0 = memT, 0
            if u < 18:
                src, gb = xp[u * 256 + tl * 128:u * 256 + (tl + 1) * 128, :], gin
            else:
                src, gb = memx[tl * 128:(tl + 1) * 128, :], gmem
            return u, tl, hTb, c0 + tl * 128, src, gb

        def S0(t):
            u, tl, hTb, c0, src, gb = tinfo(t)
            x_ = xt[t % 4]
            P.dma("sp", x_.t[:], src, x_, writes=[x_])

        def S1(t):
            x_, h_, sv = xt[t % 4], hh[t % 3], ssv[t % 3]
            P.op("act", lambda e: e.activation(out=h_.t[:], in_=x_.t[:], func=AF.Square, accum_out=sv.t[:, 0:1]),
                 reads=[x_], writes=[h_, sv])
            P.op("act", lambda e: e.activation(out=sv.t[:, 1:2], in_=sv.t[:, 0:1], func=AF.Ln, bias=EPS, scale=1.0 / D),
                 reads=[sv], writes=[sv])
            P.op("act", lambda e: e.activation(out=sv.t[:, 2:3], in_=sv.t[:, 1:2], func=AF.Exp, scale=-0.5),
                 reads=[sv], writes=[sv])

        def S2(t):
            u, tl, hTb, c0, src, gb = tinfo(t)
            x_, h_, sv = xt[t % 4], hh[t % 3], ssv[t % 3]
            P.op("dve", lambda e: e.scalar_tensor_tensor(out=h_.t[:], in0=x_.t[:], scalar=sv.t[:, 2:3], in1=gb.t[:],
                                                         op0=ALU.mult, op1=ALU.mult),
                 reads=[x_, sv, gb], writes=[h_])

        def S3(t):
            h_ = hh[t % 3]
            k = t % 2

            def tr(e):
                ins = None
                for kc in range(16):
                    ins = e.transpose(pb2(k, kc * 128, (kc + 1) * 128), h_.t[:, kc * 128:(kc + 1) * 128], ident.t[:])
                return ins
            P.op("pe", tr, reads=[h_, ident], writes=[PS[2 * k], PS[2 * k + 1]])

        def S4(t):
            u, tl, hTb, c0, src, gb = tinfo(t)
            k = t % 2
            P.op("act", lambda e: e.activation(out=hTb.t[:, 0:8, c0:c0 + 128],
                                               in_=pb2(k, 0, 1024).rearrange("p (a b) -> p a b", b=128), func=AF.Copy),
                 reads=[PS[2 * k]], writes=[hTb])
            P.op("dve", lambda e: e.tensor_copy(out=hTb.t[:, 8:16, c0:c0 + 128],
                                                in_=pb2(k, 1024, 2048).rearrange("p (a b) -> p a b", b=128)),
                 reads=[PS[2 * k + 1]], writes=[HI[id(hTb)]])

        def uinfo(u):
            if u < 12:
                return hT_rot[u % 2], 0
            if u < 16:
                return hT_own, (u - 12) * 256
            return hT_rot[u % 2], 0

        def U1(u):
            hTb, c0 = uinfo(u)
            p0, p1 = 4 + 2 * (u % 2), 5 + 2 * (u % 2)
            cs_ = cst[u % 2]
            P.dma("sp", cs_.t[:], csk[:, u * 256:(u + 1) * 256], cs_, writes=[cs_])

            def mm(e):
                ins = None
                for g in range(3):
                    dst = pf(p0, g * 256, (g + 1) * 256) if g < 2 else pf(p1, 0, 256)
                    for kc in range(16):
                        ins = e.matmul(dst, lhsT=wkv.t[:, kc, g * 128:(g + 1) * 128], rhs=hTb.t[:, kc, c0:c0 + 256],
                                       start=(kc == 0), stop=(kc == 15))
                return ins
            P.op("pe", mm, reads=[wkv, hTb, HI[id(hTb)]], writes=[PS[p0], PS[p1]])

        def U2(u):
            p0 = 4 + 2 * (u % 2)
            sq_ = sq[u % 2]
            P.op("act", lambda e: e.activation(out=sq_.t[:].rearrange("p a b -> p (a b)"), in_=pf(p0, 0, 512), func=AF.Square),
                 reads=[PS[p0]], writes=[sq_])

        def U3(u):
            p1 = 5 + 2 * (u % 2)
            sq_ = sq[u % 2]

            def mm2(e):
                e.matmul(pf(p1, 256, 512), lhsT=ones.t[:], rhs=sq_.t[:, 0, :], start=True, stop=False)
                return e.matmul(pf(p1, 256, 512), lhsT=ones.t[:], rhs=sq_.t[:, 1, :], start=False, stop=True)
            P.op("pe", mm2, reads=[ones, sq_], writes=[PS[p1]])

        def U4(u):
            p1 = 5 + 2 * (u % 2)
            r_ = rstd2[u % 2]
            P.op("act", lambda e: e.activation(out=r_.t[:], in_=pf(p1, 256, 512), func=AF.Ln, bias=EPS, scale=1.0 / 256),
                 reads=[PS[p1]], writes=[r_])
            P.op("act", lambda e: e.activation(out=r_.t[:], in_=r_.t[:], func=AF.Exp, scale=-0.5),
                 reads=[r_], writes=[r_])

        def U5(u):
            p0, p1 = 4 + 2 * (u % 2), 5 + 2 * (u % 2)
            r_, cs_, t_ = rstd2[u % 2], cst[u % 2], tt_[u % 2]
            for c in range(2):
                P.op("dve", lambda e, c=c: e.scalar_tensor_tensor(
                    out=kvnT.t[:, c, u * 256:(u + 1) * 256], in0=pf(p0, c * 256, (c + 1) * 256), scalar=gkv.t[:, c:c + 1],
                    in1=r_.t[:], op0=ALU.mult, op1=ALU.mult), reads=[PS[p0], gkv, r_], writes=[kvnT])
            P.op("dve", lambda e: e.tensor_tensor(out=t_.t[:], in0=pf(p1, 0, 256), in1=cs_.t[:], op=ALU.mult),
                 reads=[PS[p1], cs_], writes=[t_])
            P.op("pool", lambda e: e.tensor_copy(out=t2_.t[0:64, :], in_=t_.t[64:128, :]), reads=[t_], writes=[t2_])
            P.op("pool", lambda e: e.tensor_tensor(out=kpeT.t[0:64, u * 256:(u + 1) * 256], in0=t_.t[0:64, :],
                                                   in1=t2_.t[0:64, :], op=ALU.add), reads=[t_, t2_], writes=[kpeT])

        def SW(u):
            hTb, c0 = uinfo(u)
            p0, p1 = 4 + 2 * (u % 2), 5 + 2 * (u % 2)
            t0 = (u - 12) * 2 if u < 16 else 8 + (u - 16) * 2
            vvT = vvTs[u % 2]

            def mm3(e):
                ins = None
                for g in range(2):
                    dst = pf(p0 + g, 0, 256)
                    for kc in range(16):
                        ins = e.matmul(dst, lhsT=wkv.t[:, kc, (3 + g) * 128:(4 + g) * 128], rhs=hTb.t[:, kc, c0:c0 + 256],
                                       start=(kc == 0), stop=(kc == 15))
                return ins
            P.op("pe", mm3, reads=[wkv, wkv_sw, hTb, HI[id(hTb)]], writes=[PS[p0], PS[p1]])
            cs_ = slice(t0 * 128, (t0 + 2) * 128)
            for (dstb, r0) in ((kk0, 0), (kk1, 64)):
                for o0 in (0, 64):
                    P.op("dve", lambda e, dstb=dstb, r0=r0, o0=o0: e.tensor_copy(out=dstb.t[o0:o0 + 64, cs_], in_=pf(p0, 0, 256, r0, r0 + 64)),
                         reads=[PS[p0]], writes=[dstb])
            for (jj, r0) in ((0, 0), (1, 64)):
                for o0 in (0, 64):
                    P.op("act", lambda e, jj=jj, r0=r0, o0=o0: e.activation(out=vvT.t[o0:o0 + 64, jj, :], in_=pf(p1, 0, 256, r0, r0 + 64), func=AF.Copy),
                         reads=[PS[p1]], writes=[vvT])

        def SWb(u):
            p0 = 4 + 2 * (u % 2)
            t0 = (u - 12) * 2 if u < 16 else 8 + (u - 16) * 2
            vvT = vvTs[u % 2]

            def tr2(e):
                ins = None
                for tl in range(2):
                    for jj in range(2):
                        ins = e.transpose(pb(p0, (tl * 2 + jj) * 128, (tl * 2 + jj + 1) * 128),
                                          vvT.t[:, jj, tl * 128:(tl + 1) * 128], ident.t[:])
                return ins
            P.op("pe", tr2, reads=[vvT, ident], writes=[PS[p0]])
            P.op("dve", lambda e: e.tensor_copy(out=vv.t[:, t0:t0 + 2, :], in_=pb(p0, 0, 512).rearrange("p (a b) -> p a b", b=256)),
                 reads=[PS[p0]], writes=[vv])

        def unit_of(tlast):
            if tlast < 0 or tlast >= NT or tlast % 2 == 0:
                return None
            return tlast // 2

        def L(f, a):
            P.label = "%s(%d)" % (f.__name__, a)
            f(a)

        for k in range(-4, NT + 10):
            if 0 <= k + 4 < NT:
                L(S0, k + 4)
            if k == 8:
                P.label = "conb"
                load_conb()
            u = unit_of(k - 3)
            if u is not None and u < 16:
                L(U2, u)
            u = unit_of(k - 4)
            if u is not None and u < 16:
                L(U4, u)
                L(U5, u)
            if 0 <= k - 1 < NT:
                L(S4, k - 1)
            if 0 <= k + 2 < NT:
                L(S1, k + 2)
            if 0 <= k + 1 < NT:
                L(S2, k + 1)
            if 0 <= k < NT:
                L(S3, k)
            u = unit_of(k - 3)
            if u is not None and u < 16:
                L(U3, u)
            u = unit_of(k - 2)
            if u is not None and u < 16:
                L(U1, u)
            if u is not None and 16 <= u < 18:
                L(SW, u)
            u = unit_of(k - 3)
            if u is not None and 16 <= u < 18:
                L(SWb, u)
            u = unit_of(k - 4)
            if u is not None and 12 <= u < 16:
                L(SW, u)
            u = unit_of(k - 5)
            if u is not None and 12 <= u < 16:
                L(SWb, u)
        chk(4)
        P.label = "own"
        gate = P.sb([128, 16, 1024], BF16, R1)
        cqT = P.sb([128, 4, 1024], BF16, R1 + 32 * KB)
        wuq = P.sb([128, 4, 2048], BF16, R1 + 40 * KB)
        wukv = P.sb([128, 2, 2048], BF16, R1 + 56 * KB)
        qsw = P.sb([128, 4, 1024], BF16, R6b)
        qmem = P.sb([128, 4, 1024], BF16, R6b + 8 * KB)
        cqf = P.sb([128, 4, 512], F32, R7)
        sqq = P.sb([128, 4, 512], BF16, R7 + 8 * KB)
        lnq = P.sb([128, 512], F32, R7 + 12 * KB)

        otiles = [("cq", C_CQ), ("cq", C_CQ + 256)]
        otiles += [("z", C_ZMLA + 256 * i, 2 * i) for i in range(4)]
        otiles += [("qsw", C_QSWA, 0), ("qsw", C_QSWA + 256, 2)]
        otiles += [("z", C_ZSWA, 8), ("z", C_ZSWA + 256, 10)]
        otiles += [("qmem", C_QMEM, 0), ("qmem", C_QMEM + 256, 2)]
        otiles += [("z", C_ZMEM, 12), ("z", C_ZMEM + 256, 14)]
        pend = [ring_load(wv(t[1], t[1] + 256)) for t in otiles[:3]]
        chk(4.05)
        nxt = 3
        obank = [0]

        def proj_group(wb, g, half):
            b = obank[0] % 6
            obank[0] += 1

            def mm(e):
                ins = None
                for kc in range(16):
                    ins = e.matmul(pf(b, 0, 512), lhsT=wb.t[:, kc, g * 128:(g + 1) * 128],
                                   rhs=hT_own.t[:, kc, half * 512:(half + 1) * 512], start=(kc == 0), stop=(kc == 15))
                return ins
            P.op("pe", mm, reads=[wb, hT_own, hT_own_hi], writes=[PS[b]])
            return b

        wq0, wq1 = pend[0], pend[1]
        for half in range(2):
            for c in range(4):
                wb = wq0 if c < 2 else wq1
                b = proj_group(wb, c % 2, half)
                P.op("dve", lambda e, b=b, c=c: e.tensor_copy(out=cqf.t[:, c, :], in_=pf(b, 0, 512)), reads=[PS[b]], writes=[cqf])
                P.op("act", lambda e, c=c: e.activation(out=sqq.t[:, c, :], in_=cqf.t[:, c, :], func=AF.Square),
                     reads=[cqf], writes=[sqq])
                chk(4.1 + 0.01 * c + 0.04 * half)

            def mmq(e):
                ins = None
                for c in range(4):
                    ins = e.matmul(pf(6, 0, 512), lhsT=ones.t[:], rhs=sqq.t[:, c, :], start=(c == 0), stop=(c == 3))
                return ins
            P.op("pe", mmq, reads=[ones, sqq], writes=[PS[6]])
            chk(4.15 + 0.04 * half)
            P.op("act", lambda e: e.activation(out=lnq.t[:], in_=pf(6, 0, 512), func=AF.Ln, bias=EPS, scale=1.0 / 512),
                 reads=[PS[6]], writes=[lnq])
            P.op("act", lambda e: e.activation(out=lnq.t[:], in_=lnq.t[:], func=AF.Exp, scale=-0.5), reads=[lnq], writes=[lnq])
            for c in range(4):
                P.op("dve", lambda e, c=c, half=half: e.scalar_tensor_tensor(
                    out=cqT.t[:, c, half * 512:(half + 1) * 512], in0=cqf.t[:, c, :], scalar=gq.t[:, c:c + 1], in1=lnq.t[:],
                    op0=ALU.mult, op1=ALU.mult), reads=[cqf, gq, lnq], writes=[cqT])
        chk(4.2)
        pend = pend[2:]
        while len(pend) < 3 and nxt < len(otiles):
            t = otiles[nxt]; nxt += 1
            pend.append(ring_load(wv(t[1], t[1] + 256)))
        ev_i = [0]
        for ti in range(2, len(otiles)):
            kind, col, ch = otiles[ti]
            wb = pend.pop(0)
            for g in range(2):
                for half in range(2):
                    b = proj_group(wb, g, half)
                    hs = slice(half * 512, (half + 1) * 512)
                    if kind == "z":
                        P.op("act", lambda e, b=b, c=ch + g, hs=hs: e.activation(out=gate.t[:, c, hs], in_=pf(b, 0, 512), func=AF.Silu),
                             reads=[PS[b]], writes=[gate])
                    else:
                        dstb = qsw if kind == "qsw" else qmem
                        P.op("dve", lambda e, b=b, c=ch + g, hs=hs, dstb=dstb: e.tensor_copy(out=dstb.t[:, c, hs], in_=pf(b, 0, 512)),
                             reads=[PS[b]], writes=[dstb])
            chk(4.3 + 0.01 * ti)
            if nxt < len(otiles):
                t = otiles[nxt]; nxt += 1
                pend.append(ring_load(wv(t[1], t[1] + 256)))

        chk(5)
        WQ = P.mk(None)
        uqv = w_uq[:, :].rearrange("(kc p) (h c) -> p kc h c", p=128, c=192)
        wuqv = wuq.t[:].rearrange("p kc (h c) -> p kc h c", c=256)
        for kc in range(4):
            P.dma("pool", wuqv[:, kc, :, 0:192], uqv[:, kc, :, :], WQ, writes=[wuq])
            P.dma("pool", wuqv[:, kc, :, 192:224], uqv[:, kc, :, 160:192], WQ, writes=[wuq])
            P.dma("pool", wuqv[:, kc, :, 224:256], uqv[:, kc, :, 128:160], WQ, writes=[wuq])
        P.dma("pool", wukv.t[:], w_ukv[:, :].rearrange("(kc p) c -> p kc c", p=128), WQ, writes=[wukv])
        wuq.w = P.totals(WQ)
        wukv.w = P.totals(WQ)

        P.label = "swa"
        bm = P.sb([128, 2, 8, 128], F32, R5)
        swm = P.sb([128, 2, 128], F32, R5 + 8 * KB)
        ssw = [P.sb([128, 512], F32, R5 + 9 * KB + 2 * KB * i) for i in range(2)]
        psw = [P.sb([128, 512], BF16, R5 + 13 * KB + KB * i) for i in range(6)]
        sinkf = P.sb([1, 1024], F32, 187 * KB)
        sinkh = P.sb([1, 1024], BF16, 191 * KB)
        sinkl = P.sb([1, 1024], BF16, 193 * KB)
        sinkt = P.sb([1, 1024], F32, 195 * KB)
        SWB = P.mk(None)
        P.dma("sp", sinkf.t[:], sink_d[:, :], SWB, writes=[sinkf])
        P.dma("sp", bm.t[:].rearrange("p a b c -> p (a b c)"), swab_d[:, :], SWB, writes=[bm])
        P.dma("sp", swm.t[:].rearrange("p a b -> p (a b)"), swam_d[:, :], SWB, writes=[swm])
        bm.w = P.totals(SWB)
        swm.w = P.totals(SWB)
        sinkf.w = P.totals(SWB)
        P.label = "mem"
        kmT = P.sb([128, 4, 256], BF16, R7)
        vmem = P.sb([128, 2, 512], BF16, R7 + 2 * KB)
        pm = [P.sb([128, 2, 512], BF16, R7 + 4 * KB + 2 * KB * i) for i in range(2)]
        rinv = [P.sb([128, 512], F32, R7 + 8 * KB + 2 * KB * i) for i in range(2)]
        tmpo = P.sb([128, 512], F32, R7 + 12 * KB)
        mring = [ring_load(w_mkv[:, i * 256:(i + 1) * 256].rearrange("(kc p) c -> p kc c", p=128)) for i in range(3)]
        for i in range(4):
            wb = mring[i] if i < 3 else ring_load(w_mkv[:, 768:1024].rearrange("(kc p) c -> p kc c", p=128))
            if i < 2:
                for g in range(2):
                    hd = i * 2 + g
                    b = obank[0] % 6
                    obank[0] += 1

                    def mm(e, wb=wb, g=g, b=b):
                        ins = None
                        for kc in range(16):
                            ins = e.matmul(pf(b, 0, 256), lhsT=wb.t[:, kc, g * 128:(g + 1) * 128], rhs=memT.t[:, kc, :],
                                           start=(kc == 0), stop=(kc == 15))
                        return ins
                    P.op("pe", mm, reads=[wb, memT, memT_hi], writes=[PS[b]])
                    P.op("dve", lambda e, b=b, hd=hd: e.tensor_copy(out=kmT.t[:, hd, :], in_=pf(b, 0, 256)), reads=[PS[b]], writes=[kmT])
            else:
                b = obank[0] % 6
                obank[0] += 1

                def mm(e, wb=wb, b=b):
                    ins = None
                    for mt in range(2):
                        for kc in range(16):
                            ins = e.matmul(pf(b, mt * 256, (mt + 1) * 256), lhsT=memT.t[:, kc, mt * 128:(mt + 1) * 128],
                                           rhs=wb.t[:, kc, :], start=(kc == 0), stop=(kc == 15))
                    return ins
                P.op("pe", mm, reads=[wb, memT, memT_hi], writes=[PS[b]])
                cc = (i - 2) * 256
                P.op("dve", lambda e, b=b, cc=cc: e.tensor_copy(out=vmem.t[:, :, cc:cc + 256],
                                                                in_=pf(b, 0, 512).rearrange("p (a b) -> p a b", b=256)),
                     reads=[PS[b]], writes=[vmem])

        def finish(obk, lbk, ncol, pi, mixdst_fn):
            rv = rinv[pi % 2]
            P.op("act", lambda e: e.activation(out=rv.t[:, 0:ncol], in_=pf(lbk, 0, ncol), func=AF.Ln), reads=[PS[lbk]], writes=[rv])
            P.op("act", lambda e: e.activation(out=rv.t[:, 0:ncol], in_=rv.t[:, 0:ncol], func=AF.Exp, scale=-1.0), reads=[rv], writes=[rv])
            P.op("dve", lambda e: e.tensor_tensor(out=tmpo.t[:, 0:ncol], in0=pf(obk, 0, ncol), in1=rv.t[:, 0:ncol], op=ALU.mult),
                 reads=[PS[obk], rv], writes=[tmpo])
            mixdst_fn()

        def memA(it):
            hd, half = it // 2, it % 2
            hs = slice(half * 512, (half + 1) * 512)
            sb_ = [(it % 2) * 2, (it % 2) * 2 + 1]
            pmb = pm[it % 2]

            def mms(e):
                ins = None
                for mt in range(2):
                    ins = e.matmul(pf(sb_[mt], 0, 512), lhsT=kmT.t[:, hd, mt * 128:(mt + 1) * 128], rhs=qmem.t[:, hd, hs],
                                   start=True, stop=True)
                return ins
            P.op("pe", mms, reads=[kmT, qmem], writes=[PS[sb_[0]], PS[sb_[1]]])
            for mt in range(2):
                P.op("act", lambda e, mt=mt: e.activation(out=pmb.t[:, mt, :], in_=pf(sb_[mt], 0, 512),
                                                          func=AF.Exp, scale=MEM_SCALE),
                     reads=[PS[sb_[mt]]], writes=[pmb])

        def memB(it):
            hd, half = it // 2, it % 2
            hs = slice(half * 512, (half + 1) * 512)
            ob, lb = 4 + (it % 2), 6 + (it % 2)
            pmb = pm[it % 2]

            def mmo(e):
                for mt in range(2):
                    e.matmul(pf(ob, 0, 512), lhsT=vmem.t[:, mt, hd * 128:(hd + 1) * 128], rhs=pmb.t[:, mt, :],
                             start=(mt == 0), stop=(mt == 1))
                ins = None
                for mt in range(2):
                    ins = e.matmul(pf(lb, 0, 512), lhsT=ones.t[:], rhs=pmb.t[:, mt, :], start=(mt == 0), stop=(mt == 1))
                return ins
            P.op("pe", mmo, reads=[vmem, pmb, ones], writes=[PS[ob], PS[lb]])

            def mixm():
                P.op("pool", lambda e: e.tensor_tensor(out=gate.t[:, 12 + hd, hs], in0=tmpo.t[:], in1=gate.t[:, 12 + hd, hs], op=ALU.mult),
                     reads=[tmpo, gate], writes=[gate])
            finish(ob, lb, 512, it, mixm)

        for k in range(9):
            if k < 8:
                memA(k)
            if k >= 1:
                memB(k - 1)

        chk(6)
        P.label = "swa"
        for kt in range(2):
            P.op("pool", lambda e, kt=kt: e.tensor_tensor(out=bm.t[:, kt, :, :], in0=bm.t[:, kt, :, :],
                                                          in1=swm.t[:, kt:kt + 1, :].to_broadcast([128, 8, 128]), op=ALU.add),
                 reads=[bm, swm], writes=[bm])
        P.op("act", lambda e: e.activation(out=sinkt.t[:], in_=sinkf.t[:], func=AF.Exp), reads=[sinkf], writes=[sinkt])
        P.op("dve", lambda e: e.tensor_copy(out=sinkh.t[:], in_=sinkt.t[:]), reads=[sinkt], writes=[sinkh])
        P.op("dve", lambda e: e.tensor_copy(out=sinkf.t[:], in_=sinkh.t[:]), reads=[sinkh], writes=[sinkf])
        P.op("dve", lambda e: e.tensor_tensor(out=sinkl.t[:], in0=sinkt.t[:], in1=sinkf.t[:], op=ALU.subtract),
             reads=[sinkt, sinkf], writes=[sinkl])

        def swinfo(it):
            qb, kvh = it // 2, it % 2
            ui = qb // 2
            tiles = [(8 + ui) if qb % 2 == 0 else (qb - 1), qb]
            return qb, kvh, tiles, (kk0 if kvh == 0 else kk1)

        def swaA(it, part):
            qb, kvh, tiles, kk = swinfo(it)

            def mms(e):
                ins = None
                for kt in range(2):
                    for hq in range(4):
                        hd = kvh * 4 + hq
                        par, hq2 = hq % 2, hq // 2
                        g, pr = hd // 2, par * 64
                        ins = e.matmul(pf(kt * 2 + par, hq2 * 128, (hq2 + 1) * 128),
                                       lhsT=kk.t[pr:pr + 64, tiles[kt] * 128:(tiles[kt] + 1) * 128],
                                       rhs=qsw.t[pr:pr + 64, g, qb * 128:(qb + 1) * 128], start=True, stop=True)
                return ins
            if part == 1:
                P.op("pe", mms, reads=[kk, qsw], writes=[PS[0], PS[1], PS[2], PS[3]])
            for kt in range(2):
                sw_ = ssw[kt]
                pb_ = psw[(it % 3) * 2 + kt]
                if part == 1:
                    P.op("dve", lambda e, kt=kt, sw_=sw_: e.scalar_tensor_tensor(
                        out=sw_.t[:].rearrange("p (a b) -> p a b", a=2),
                        in0=DB[kt][:, :].rearrange("p (b c) -> p b c", b=2)[:, :, 0:256], scalar=0.125,
                        in1=bm.t[:, kt, kvh * 4:(kvh + 1) * 4, :].rearrange("p (a b) c -> p a (b c)", a=2), op0=ALU.mult, op1=ALU.add),
                        reads=[PS[kt * 2], PS[kt * 2 + 1], bm], writes=[sw_])
                    continue
                if kt == 0:
                    P.op("act", lambda e, sw_=sw_, pb_=pb_: e.activation(out=pb_.t[:], in_=sw_.t[:], func=AF.Exp,
                                                                          bias=kvb.t[:, qb:qb + 1], scale=1.0),
                         reads=[sw_, kvb], writes=[pb_])
                else:
                    P.op("act", lambda e, sw_=sw_, pb_=pb_: e.activation(out=pb_.t[:], in_=sw_.t[:], func=AF.Exp),
                         reads=[sw_], writes=[pb_])

        def swaB(it):
            qb, kvh, tiles, kk = swinfo(it)
            ob, lb = 4 + (it % 2), 6 + (it % 2)
            pbs = [psw[(it % 3) * 2 + kt] for kt in range(2)]

            def mmo(e):
                for kt in range(2):
                    e.matmul(pf(ob, 0, 512), lhsT=vv.t[:, tiles[kt], kvh * 128:(kvh + 1) * 128], rhs=pbs[kt].t[:],
                             start=(kt == 0), stop=(kt == 1))
                for kt in range(2):
                    e.matmul(pf(lb, 0, 512), lhsT=ones.t[:], rhs=pbs[kt].t[:], start=(kt == 0), stop=False)
                e.matmul(pf(lb, 0, 512), lhsT=ones.t[0:1, :], rhs=sinkh.t[0:1, kvh * 512:(kvh + 1) * 512], start=False, stop=False)
                return e.matmul(pf(lb, 0, 512), lhsT=ones.t[0:1, :], rhs=sinkl.t[0:1, kvh * 512:(kvh + 1) * 512], start=False, stop=True)
            P.op("pe", mmo, reads=[vv, ones, sinkh, sinkl] + pbs, writes=[PS[ob], PS[lb]])

            def mixs():
                for par in range(2):
                    pr = par * 64
                    c8 = 8 + kvh * 2
                    P.op("pool", lambda e, par=par, pr=pr, c8=c8: e.tensor_tensor(
                        out=gate.t[pr:pr + 64, c8:c8 + 2, qb * 128:(qb + 1) * 128],
                        in0=tmpo.t[pr:pr + 64, par * 256:(par + 1) * 256].rearrange("p (a b) -> p a b", a=2),
                        in1=gate.t[pr:pr + 64, c8:c8 + 2, qb * 128:(qb + 1) * 128], op=ALU.mult), reads=[tmpo, gate], writes=[gate])
            finish(ob, lb, 512, it, mixs)

        for k in range(18):
            if k < 16:
                swaA(k, 1)
            if k >= 2:
                swaB(k - 2)
            if k < 16:
                swaA(k, 2)

        chk(7)
        P.label = "mla"
        Kb = [P.sb([128, S], BF16, R5 + 8 * KB * i) for i in range(2)]
        Vb = [P.sb([128, 32, 128], BF16, R5 + 16 * KB + 8 * KB * i) for i in range(2)]
        Qn = [P.sb([128, 1024], BF16, R6 + 2 * KB * i) for i in range(2)]
        Qp = [P.sb([128, 1024], BF16, R6 + 4 * KB + 2 * KB * i) for i in range(2)]
        csq = P.sb([128, 1024], F32, R6 + 8 * KB)
        tq = P.sb([128, 512], F32, R6 + 12 * KB)
        tq2 = P.sb([128, 512], F32, R6 + 14 * KB)
        pmla = [P.sb([128, 512], BF16, R6 + 16 * KB + KB * i) for i in range(3)]
        p2s = [P.sb([128, 256], BF16, R6 + 22 * KB + 512 * i) for i in range(3)]
        rinvm = [P.sb([128, 256], F32, R6 + 19 * KB + KB * i) for i in range(2)]
        tmpm = P.sb([128, 256], F32, R6 + 21 * KB)
        QC = P.mk(None)
        P.dma("sp", csq.t[:], csk[:, 3072:4096], QC, writes=[csq])
        for i in range(2):
            P.dma("pool", Qp[i].t[64:128, :], gtm[:, :], QC, writes=[Qp[i]])
        for b_ in (csq, Qp[0], Qp[1]):
            b_.w = b_.w + P.totals(QC)

        GEN_BANK = 7

        def gen_chunks(h, banks=(7,)):
            s = h % 2
            K_, V_, Qn_, Qp_ = Kb[s], Vb[s], Qn[s], Qp[s]
            ch = []
            for half in range(2):
                gb = banks[len(ch) % len(banks)]
                def qn(half=half, gb=gb):
                    def mm(e):
                        ins = None
                        for kc in range(4):
                            ins = e.matmul(pf(gb, 0, 512), lhsT=wuq.t[:, kc, h * 256:h * 256 + 128],
                                           rhs=cqT.t[:, kc, half * 512:(half + 1) * 512], start=(kc == 0), stop=(kc == 3))
                        return ins
                    P.op("pe", mm, reads=[wuq, cqT], writes=[PS[gb]])
                    P.op("dve", lambda e: e.tensor_copy(out=Qn_.t[:, half * 512:(half + 1) * 512], in_=pf(gb, 0, 512)),
                         reads=[PS[gb]], writes=[Qn_])
                ch.append(qn)

                gb = banks[len(ch) % len(banks)]
                def qp(half=half, gb=gb):
                    def mm(e):
                        ins = None
                        for kc in range(4):
                            ins = e.matmul(pf(gb, 0, 512), lhsT=wuq.t[:, kc, h * 256 + 128:h * 256 + 256],
                                           rhs=cqT.t[:, kc, half * 512:(half + 1) * 512], start=(kc == 0), stop=(kc == 3))
                        return ins
                    P.op("pe", mm, reads=[wuq, cqT], writes=[PS[gb]])
                    P.op("dve", lambda e: e.tensor_tensor(out=tq.t[:], in0=pf(gb, 0, 512), in1=csq.t[:, half * 512:(half + 1) * 512], op=ALU.mult),
                         reads=[PS[gb], csq], writes=[tq])
                    P.op("pool", lambda e: e.tensor_copy(out=tq2.t[0:64, :], in_=tq.t[64:128, :]), reads=[tq], writes=[tq2])
                    P.op("pool", lambda e: e.tensor_tensor(out=Qp_.t[0:64, half * 512:(half + 1) * 512], in0=tq.t[0:64, :], in1=tq2.t[0:64, :], op=ALU.add),
                         reads=[tq, tq2], writes=[Qp_])
                ch.append(qp)
            for c8 in range(8):
                gb = banks[len(ch) % len(banks)]
                def kg(c8=c8, gb=gb):
                    def mm(e):
                        ins = None
                        for kc in range(2):
                            ins = e.matmul(pf(gb, 0, 512), lhsT=wukv.t[:, kc, h * 256:h * 256 + 128],
                                           rhs=kvnT.t[:, kc, c8 * 512:(c8 + 1) * 512], start=(kc == 0), stop=(kc == 1))
                        return ins
                    P.op("pe", mm, reads=[wukv, kvnT], writes=[PS[gb]])
                    P.op("dve", lambda e: e.tensor_copy(out=K_.t[:, c8 * 512:(c8 + 1) * 512], in_=pf(gb, 0, 512)),
                         reads=[PS[gb]], writes=[K_])
                ch.append(kg)

                gb = banks[len(ch) % len(banks)]
                def vg(c8=c8, gb=gb):
                    def mm(e):
                        ins = None
                        for t4 in range(4):
                            tl = c8 * 4 + t4
                            for kc in range(2):
                                ins = e.matmul(pf(gb, t4 * 128, (t4 + 1) * 128), lhsT=kvnT.t[:, kc, tl * 128:(tl + 1) * 128],
                                               rhs=wukv.t[:, kc, h * 256 + 128:h * 256 + 256], start=(kc == 0), stop=(kc == 1))
                        return ins
                    P.op("pe", mm, reads=[wukv, kvnT], writes=[PS[gb]])
                    P.op("dve", lambda e: e.tensor_copy(out=V_.t[:, c8 * 4:(c8 + 1) * 4, :], in_=pf(gb, 0, 512).rearrange("p (a b) -> p a b", b=128)),
                         reads=[PS[gb]], writes=[V_])
                ch.append(vg)
            return ch

        for c in gen_chunks(0, banks=(0, 1, 2, 3, 4, 5, 6, 7)):
            c()

        kpeT.w = kpeT.w + conb_tot
        P.label = "outpre"
        wo_t = [ring_load(w_out[:, i * 256:(i + 1) * 256].rearrange("(kc p) c -> p kc c", p=128)) for i in range(3)]
        P.label = "mla"
        steps = []
        for h in range(8):
            for i in range(4):
                kus = list(range(3 * (i + 1))) + list(range(12, 13 + i))
                for n, ku in enumerate(kus):
                    steps.append((h, i, ku, n == 0, n == len(kus) - 1))
        SKEW = 2
        pendq = []
        sidx = [0]
        gen_next = []
        cur_h = -1

        def rec_pv(item):
            (h, i, ku, first, last, pbuf, p2) = item
            s = h % 2
            ob, lb = 3 + (i % 2), 5 + (i % 2)

            def mm(e):
                for t in range(2):
                    tl = ku * 2 + t
                    e.matmul(pf(ob, 0, 256), lhsT=Vb[s].t[:, tl, :], rhs=pbuf.t[:, t * 256:(t + 1) * 256],
                             start=(first and t == 0), stop=(last and t == 1))
                return e.matmul(pf(lb, 0, 256), lhsT=ones.t[:], rhs=p2.t[:], start=first, stop=last)
            P.op("pe", mm, reads=[Vb[s], pbuf, p2, ones], writes=[PS[ob], PS[lb]])
            if last:
                rv = rinvm[i % 2]
                P.op("act", lambda e: e.activation(out=rv.t[:], in_=pf(lb, 0, 256), func=AF.Ln), reads=[PS[lb]], writes=[rv])
                P.op("act", lambda e: e.activation(out=rv.t[:], in_=rv.t[:], func=AF.Exp, scale=-1.0), reads=[rv], writes=[rv])
                P.op("dve", lambda e: e.tensor_tensor(out=tmpm.t[:], in0=pf(ob, 0, 256), in1=rv.t[:], op=ALU.mult),
                     reads=[PS[ob], rv], writes=[tmpm])
                P.op("pool", lambda e: e.tensor_tensor(out=gate.t[:, h, i * 256:(i + 1) * 256], in0=tmpm.t[:],
                                                       in1=gate.t[:, h, i * 256:(i + 1) * 256], op=ALU.mult),
                     reads=[tmpm, gate], writes=[gate])

        for (h, i, ku, first, last) in steps:
            if h != cur_h:
                assert len(pendq) <= SKEW
                cur_h = h
                gen_next = gen_chunks(h + 1) if h + 1 < 8 else []
            s = h % 2
            sbank = sidx[0] % 3
            pbuf = pmla[sidx[0] % 3]
            sidx[0] += 1

            def mms(e, s=s, i=i, ku=ku, sbank=sbank):
                ins = None
                for t in range(2):
                    kc0 = ku * 256 + t * 128
                    e.matmul(pf(sbank, t * 256, (t + 1) * 256), lhsT=Kb[s].t[:, kc0:kc0 + 128], rhs=Qn[s].t[:, i * 256:(i + 1) * 256],
                             start=True, stop=False)
                    ins = e.matmul(pf(sbank, t * 256, (t + 1) * 256), lhsT=kpeT.t[:, kc0:kc0 + 128], rhs=Qp[s].t[:, i * 256:(i + 1) * 256],
                                   start=False, stop=True)
                return ins
            P.op("pe", mms, reads=[Kb[s], Qn[s], Qp[s], kpeT], writes=[PS[sbank]])
            P.op("act", lambda e, sbank=sbank, pbuf=pbuf: e.activation(out=pbuf.t[:], in_=pf(sbank, 0, 512), func=AF.Exp, scale=MLA_SCALE),
                 reads=[PS[sbank]], writes=[pbuf])
            p2 = p2s[(sidx[0] - 1) % 3]
            P.op("dve", lambda e, pbuf=pbuf, p2=p2: e.tensor_tensor(out=p2.t[:], in0=pbuf.t[:, 0:256], in1=pbuf.t[:, 256:512], op=ALU.add),
                 reads=[pbuf], writes=[p2])
            pendq.append((h, i, ku, first, last, pbuf, p2))
            if len(pendq) > SKEW:
                rec_pv(pendq.pop(0))
            if gen_next and (sidx[0] % 2 == 0):
                gen_next.pop(0)()
        while pendq:
            rec_pv(pendq.pop(0))
        assert not gen_next

        chk(8)
        P.label = "out"
        res = P.sb([128, 8, D], F32, R5)
        resb = [P.mk(res.t, fresh=False) for _ in range(8)]
        gfin = P.sb([128, D], F32, 177 * KB)
        xo = ([P.sb([128, D], F32, R1 + 32 * KB + 8 * KB * i) for i in range(4)]
              + [P.sb([128, D], F32, 185 * KB), P.sb([128, D], F32, 193 * KB)]
              + [P.sb([128, D], F32, 1 * KB), P.sb([128, D], F32, 9 * KB)])
        GF = P.mk(None)
        for tt in range(8):
            P.dma("sp", xo[tt].t[:], xp[3072 + tt * 128:3072 + (tt + 1) * 128, :], xo[tt], writes=[xo[tt]])
        P.dma("sp", gfin.t[:], gfin_d[:, :], GF, writes=[gfin])
        obk = [0]
        NFUSE = 2

        def F1(tt):
            x_, r_ = xo[tt], resb[tt]
            sv = ssv[tt % 4]
            P.op("dve", lambda e: e.tensor_tensor(out=res.t[:, tt, 0:NFUSE * 256], in0=res.t[:, tt, 0:NFUSE * 256],
                                                  in1=x_.t[:, 0:NFUSE * 256], op=ALU.add),
                 reads=[x_], writes=[r_])
            P.op("act", lambda e: e.activation(out=x_.t[:], in_=res.t[:, tt, :], func=AF.Square, accum_out=sv.t[:, 0:1]),
                 reads=[r_], writes=[x_, sv])
            P.op("act", lambda e: e.activation(out=sv.t[:, 1:2], in_=sv.t[:, 0:1], func=AF.Ln, bias=EPS, scale=1.0 / D),
                 reads=[sv], writes=[sv])
            P.op("act", lambda e: e.activation(out=sv.t[:, 2:3], in_=sv.t[:, 1:2], func=AF.Exp, scale=-0.5),
                 reads=[sv], writes=[sv])

        def F2(tt):
            r_ = resb[tt]
            sv = ssv[tt % 4]
            P.op("dve", lambda e: e.scalar_tensor_tensor(out=res.t[:, tt, :], in0=res.t[:, tt, :], scalar=sv.t[:, 2:3], in1=gfin.t[:],
                                                         op0=ALU.mult, op1=ALU.mult),
                 reads=[sv, gfin], writes=[r_])
            P.dma("sp", out_d[tt * 128:(tt + 1) * 128, :], res.t[:, tt, :], OUTS[tt % 2], reads=[r_])

        def fin_items(t0, n=4):
            items = []
            for k in range(n + 1):
                it = []
                if k < n:
                    it.append((F1, t0 + k))
                if k >= 1:
                    it.append((F2, t0 + k - 1))
                items.append(it)
            return items

        nload = [3]
        for ci in range(8):
            P.label = "out"
            wb = wo_t.pop(0)
            for tt in range(8):
                b = obk[0] % 8
                obk[0] += 1

                def mm(e, wb=wb, tt=tt, b=b):
                    ins = None
                    for kc in range(16):
                        ins = e.matmul(pf(b, 0, 256), lhsT=gate.t[:, kc, tt * 128:(tt + 1) * 128], rhs=wb.t[:, kc, :],
                                       start=(kc == 0), stop=(kc == 15))
                    return ins
                P.op("pe", mm, reads=[wb, gate], writes=[PS[b]])
                cs_ = slice(ci * 256, (ci + 1) * 256)
                if ci >= NFUSE:
                    P.op("dve", lambda e, b=b, tt=tt, cs_=cs_: e.tensor_tensor(out=res.t[:, tt, cs_], in0=pf(b, 0, 256),
                                                                               in1=xo[tt].t[:, cs_], op=ALU.add),
                         reads=[PS[b], xo[tt]], writes=[resb[tt]])
                elif tt % 2 == 0:
                    P.op("act", lambda e, b=b, tt=tt, cs_=cs_: e.activation(out=res.t[:, tt, cs_], in_=pf(b, 0, 256), func=AF.Copy),
                         reads=[PS[b]], writes=[resb[tt]])
                else:
                    P.op("dve", lambda e, b=b, tt=tt, cs_=cs_: e.tensor_copy(out=res.t[:, tt, cs_], in_=pf(b, 0, 256)),
                         reads=[PS[b]], writes=[resb[tt]])
            if nload[0] < 8:
                cn = nload[0] % 8
                nload[0] += 1
                wo_t.append(ring_load(w_out[:, cn * 256:(cn + 1) * 256].rearrange("(kc p) c -> p kc c", p=128)))
        P.label = "fin"
        for it in fin_items(0, 8):
            for f, a in it:
                f(a)
    try:
        record()
    except _Stop:
        P.dma("sp", out_d[0:128, :], kvnT.t[:, 0, 0:1024].bitcast(F32).rearrange("p (a b) -> p a b", a=1)[:, 0, :] if False else kvnT.t[:].rearrange("p a b -> p (a b)").bitcast(F32)[:, 0:2048], OUTS[0], reads=[kvnT, kpeT])
    fin = []
    for ob_ in OUTS:
        fin += P.totals(ob_)

    blk = es.enter_context(nc.Block())

    @blk.sync
    def _(e):
        P.replay("sp", e)
        for k, v in fin:
            e.wait_ge(P.sem[k], v)

    @blk.gpsimd
    def _(e):
        P.replay("pool", e)

    @blk.scalar
    def _(e):
        P.replay("act", e)

    @blk.vector
    def _(e):
        P.replay("dve", e)

    @blk.tensor
    def _(e):
        P.replay("pe", e)

    es.close()
    nc._prog_labels = P.labels
    return nc


def _t5_bucket(rel):
    nb = 16
    max_exact = 8
    bucket = np.where(rel > 0, nb, 0)
    n = np.abs(rel)
    nf = np.maximum(n, 1).astype(np.float32)
    large = max_exact + (np.log(nf / max_exact) / np.log(128 / max_exact) * (nb - max_exact)).astype(np.int32)
    large = np.minimum(large, nb - 1)
    return bucket + np.where(n < max_exact, n, large)


def _own_units(j):
    return [j, 7 - j, 8 + j, 15 - j]


_NC_CACHE = {}


def _prepare(x, mem, norm_in, w_in, norm_q, norm_kv, w_uq, w_ukv, attn_sinks, rel_bias,
             norm_mem, w_mem_kv, w_out, norm_final):
    f32 = np.float32
    x = np.asarray(x, f32); mem = np.asarray(mem, f32)
    w_in0 = np.ascontiguousarray(np.asarray(w_in, f32)[0])
    w_uq0 = np.ascontiguousarray(np.asarray(w_uq, f32)[0])
    w_ukv0 = np.ascontiguousarray(np.asarray(w_ukv, f32)[0])
    w_mkv0 = np.ascontiguousarray(np.asarray(w_mem_kv, f32)[0])
    w_out0 = np.ascontiguousarray(np.asarray(w_out, f32)[0])
    bc = lambda v: np.ascontiguousarray(np.broadcast_to(np.asarray(v, f32).reshape(1, -1), (128, np.asarray(v).size)))
    gin = bc(norm_in[0]); gmem = bc(norm_mem[0]); gfin = bc(norm_final)
    gq = np.ascontiguousarray(np.asarray(norm_q, f32)[0].reshape(4, 128).T)
    gkv = np.ascontiguousarray(np.asarray(norm_kv, f32)[0].reshape(2, 128).T)
    inv = (1.0 / (np.float32(10000.0) ** (np.arange(0, 64, 2, dtype=f32) / np.float32(64)))).astype(f32)
    ang = (np.arange(S, dtype=f32)[:, None] * inv[None, :]).astype(f32)
    cos = np.cos(ang).astype(f32).T
    sin = np.sin(ang).astype(f32).T
    cs_nat = np.concatenate([cos, cos, -sin, sin], axis=0)
    qi = np.arange(128); kj = np.arange(256)
    rel = kj[None, :] - 128 - qi[:, None]
    bidx = _t5_bucket(rel)
    rb = np.asarray(rel_bias, f32)
    bias_qkh = rb[bidx]
    hperm = [kvh * 4 + hq2 * 2 + par for kvh in range(2) for par in range(2) for hq2 in range(2)]
    swab = np.ascontiguousarray(bias_qkh[:, :, hperm].transpose(1, 2, 0).reshape(2, 128, 8, 128).transpose(1, 0, 2, 3)).reshape(128, 2048)
    dq = (kj[None, :] // 64) - (qi[:, None] // 64)
    valid = (dq >= 0) & (dq <= 2)
    swam = np.where(valid, 0.0, -BIG).astype(f32)
    swam = np.ascontiguousarray(swam.T.reshape(2, 128, 128).transpose(1, 0, 2)).reshape(128, 256)
    sinkr = np.ascontiguousarray(np.repeat(np.asarray(attn_sinks, f32)[0][hperm], 128).reshape(1, 1024))
    ident = np.eye(128, dtype=f32)

    in_maps = []
    meta = []
    for core in range(8):
        b, j = core // 4, core % 4
        own = _own_units(j)
        others = [u for u in range(16) if u not in own]
        order = others + own
        tok = np.concatenate([np.arange(u * 256, (u + 1) * 256) for u in order])
        own_tok = tok[3072:]
        xb = x[b]
        pred = np.zeros((512, D), f32)
        kvbv = np.zeros((128, 8), f32)
        for i, u in enumerate(own):
            if u > 0:
                pred[i * 128:(i + 1) * 128] = xb[u * 256 - 128:u * 256]
            else:
                kvbv[:, 2 * i] = -BIG
        xpm = np.concatenate([xb[tok], pred], axis=0)
        chunk = tok // 64
        ohk = np.zeros((64, S), f32)
        ohk[chunk, np.arange(S)] = 1.0
        qchunk = own_tok // 64
        gt = np.where(np.arange(64)[:, None] > qchunk[None, :], -BIG, 0.0).astype(f32)
        in_maps.append({
            "xp": np.ascontiguousarray(xpm), "memx": np.ascontiguousarray(mem[b]),
            "csk": np.ascontiguousarray(cs_nat[:, tok]), "ohk": ohk, "gtm": gt,
            "w_in": w_in0, "w_uq": w_uq0, "w_ukv": w_ukv0, "w_mkv": w_mkv0, "w_out": w_out0,
            "gin": gin, "gmem": gmem, "gfin": gfin, "gq": gq, "gkv": gkv,
            "swab": swab, "swam": swam, "kvb": kvbv, "sinkr": sinkr, "ident": ident,
        })
        meta.append((b, own_tok))
    return in_maps, meta


def kernel(x, mem, norm_in, w_in, norm_q, norm_kv, w_uq, w_ukv, attn_sinks, rel_bias,
           norm_mem, w_mem_kv, w_out, norm_final):
    in_maps, meta = _prepare(x, mem, norm_in, w_in, norm_q, norm_kv, w_uq, w_ukv, attn_sinks, rel_bias,
                             norm_mem, w_mem_kv, w_out, norm_final)
    f32 = np.float32
    if "nc" not in _NC_CACHE:
        _NC_CACHE["nc"] = build()
    res = run_bass_kernel_spmd(_NC_CACHE["nc"], in_maps, core_ids=list(range(8)))
    out = np.zeros((NB, S, D), f32)
    for core in range(8):
        b, own_tok = meta[core]
        out[b, own_tok] = res.results[core]["out"]
    return out
```

```python
import contextlib
import numpy as np
import concourse.bass as bass
import concourse.mybir as mybir
from concourse.bass_utils import run_bass_kernel_spmd

F32 = mybir.dt.float32
BF16 = mybir.dt.bfloat16
AF = mybir.ActivationFunctionType
ALU = mybir.AluOpType

D = 2048
S = 4096
NB = 2
KB = 1024
SB0 = 16512
EPS = 1e-6
BIG = 30000.0
MLA_SCALE = 192.0 ** -0.5
MEM_SCALE = 128.0 ** -0.5

C_CQ, C_CKV, C_KPE, C_ZMLA, C_QSWA, C_KSWA, C_VSWA, C_ZSWA, C_QMEM, C_ZMEM = (
    0, 512, 768, 832, 1856, 2368, 2496, 2624, 3136, 3648)


class Buf:
    __slots__ = ("t", "w", "r", "dkey", "dcnt", "ps")


class Prog:
    ENG = ("pe", "act", "dve", "pool", "sp")
    CE = ("pe", "act", "dve", "pool")

    def __init__(self, nc, es):
        self.nc = nc
        self.es = es
        self.ops = {e: [] for e in self.ENG}
        self.cnt = {e: 0 for e in self.CE}
        self.known = {e: {} for e in self.ENG}
        self.sem = {}
        for e in self.CE:
            self.sem[e] = es.enter_context(nc.semaphore("sem_" + e))
        self.nd = 0
        self.nt = 0
        self.label = ""
        self.labels = {e: [] for e in self.ENG}
        self.oplabels = {e: [] for e in self.ENG}

    def sb(self, shape, dtype, off, fresh=False):
        self.nt += 1
        t = self.nc.alloc_sbuf_tensor_at("sb%d" % self.nt, list(shape), dtype, offset=SB0 + off)
        return self.mk(t, fresh)

    def mk(self, t, fresh=True):
        b = Buf()
        b.t = t
        b.w = [] if fresh else [(e, c) for e, c in self.cnt.items() if c > 0]
        b.r = []
        b.dkey = None
        b.dcnt = 0
        b.ps = False
        return b

    def _waits(self, eng, reads, writes, skip=None):
        need = {}

        def add(ev):
            k, v = ev
            if k == skip:
                return
            if v > need.get(k, 0):
                need[k] = v
        for b in reads:
            for ev in b.w:
                add(ev)
        for b in writes:
            for ev in b.w:
                add(ev)
            for ev in b.r:
                add(ev)
        kn = self.known[eng]
        wl = []
        for k, v in need.items():
            if eng == "pe" and k == "pe":
                continue
            if kn.get(k, 0) >= v:
                continue
            kn[k] = v
            wl.append((k, v))
        return wl

    def op(self, eng, fn, reads=(), writes=()):
        psr = [b for b in reads if b.ps]
        if psr:
            reads = [b for b in reads if not b.ps]
            writes = list(writes) + [b for b in psr if b not in writes]
        wl = self._waits(eng, reads, writes)
        self.cnt[eng] += 1
        ev = (eng, self.cnt[eng])
        self.ops[eng].append((wl, fn, eng))
        self.oplabels[eng].append(self.label)
        for b in reads:
            b.r.append(ev)
        for b in writes:
            b.w = [ev]
            b.r = []
        return ev

    def dma(self, q, out_ap, in_ap, owner, reads=(), writes=()):
        if owner.dkey is None:
            owner.dkey = {}
            owner.dcnt = {}
        if q not in owner.dkey:
            self.nd += 1
            owner.dkey[q] = ("d", self.nd)
            owner.dcnt[q] = 0
            self.sem[owner.dkey[q]] = self.es.enter_context(self.nc.semaphore("dsem%d" % self.nd))
        key = owner.dkey[q]
        wl = self._waits(q, reads, writes, skip=key)
        owner.dcnt[q] += 16
        ev = (key, owner.dcnt[q])
        self.ops[q].append((wl, lambda e: e.dma_start(out=out_ap, in_=in_ap), key))
        self.oplabels[q].append(self.label + "/dma")
        for b in reads:
            b.r.append(ev)
        for b in writes:
            b.w = [ev]
            b.r = []
        return ev

    def totals(self, owner):
        return [(owner.dkey[q], owner.dcnt[q]) for q in (owner.dkey or {})]

    def replay(self, eng, handle):
        class _Cnt:
            def __init__(s_, h):
                s_.h = h
                s_.n = 0

            def __getattr__(s_, name):
                a = getattr(s_.h, name)
                if callable(a):
                    def w(*args, **kw):
                        s_.n += 1
                        return a(*args, **kw)
                    return w
                return a
        for (wl, fn, sig), lab in zip(self.ops[eng], self.oplabels[eng]):
            for k, v in wl:
                handle.wait_ge(self.sem[k], v)
            c = _Cnt(handle)
            ins = fn(c)
            self.labels[eng].append((lab, c.n, [(str(k), v) for k, v in wl]))
            if sig is None:
                continue
            if isinstance(sig, tuple):
                ins.then_inc(self.sem[sig], 16)
            else:
                ins.then_inc(self.sem[sig], 1)


class _Stop(Exception):
    pass


def build(stop=99):
    nc = bass.Bass("TRN2", target_bir_lowering=False)
    es = contextlib.ExitStack()
    P = Prog(nc, es)

    def din(name, shape):
        return nc.dram_tensor(name, list(shape), F32, kind="ExternalInput").ap()

    xp = din("xp", [S + 512, D])
    memx = din("memx", [256, D])
    csk = din("csk", [128, S])
    ohk = din("ohk", [64, S])
    gtm = din("gtm", [64, 1024])
    w_in = din("w_in", [D, 4160])
    w_uq = din("w_uq", [512, 1536])
    w_ukv = din("w_ukv", [256, 2048])
    w_mkv = din("w_mkv", [D, 1024])
    w_out = din("w_out", [D, D])
    gin_d = din("gin", [128, D])
    gmem_d = din("gmem", [128, D])
    gfin_d = din("gfin", [128, D])
    gq_d = din("gq", [128, 4])
    gkv_d = din("gkv", [128, 2])
    swab_d = din("swab", [128, 2 * 8 * 128])
    swam_d = din("swam", [128, 2 * 128])
    kvb_d = din("kvb", [128, 8])
    sink_d = din("sinkr", [1, 1024])
    ident_d = din("ident", [128, 128])
    out_d = nc.dram_tensor("out", [1024, D], F32, kind="ExternalOutput").ap()

    DB = [es.enter_context(nc.psum_tensor("psd%d" % i, [128, 1024], F32)) for i in range(4)]
    PS = [P.mk(None) for _ in range(8)]
    for b_ in PS:
        b_.ps = True

    def pf(b, c0, c1, p0=0, p1=128):
        o = (b % 2) * 512
        return DB[b // 2][p0:p1, o + c0:o + c1]

    def pb2(k, c0, c1):
        return DB[k][:].bitcast(BF16)[:, c0:c1]

    def pb(b, c0, c1):
        o = (b % 2) * 1024
        return DB[b // 2][:].bitcast(BF16)[:, o + c0:o + c1]

    o = 0
    ident = P.sb([128, 128], BF16, o, True); o += 256
    ones = P.sb([128, 128], BF16, o, True); o += 256
    gq = P.sb([128, 4], F32, o, True); o += 32
    gkv = P.sb([128, 2], F32, o, True); o += 32
    kvb = P.sb([128, 8], F32, o, True); o += 32
    ssv = [P.sb([128, 4], F32, o + 32 * i, True) for i in range(4)]; o += 128
    assert o <= 1 * KB
    R0 = 1 * KB
    kvnT = P.sb([128, 2, S], BF16, R0, True)
    kpeT = P.sb([128, S], BF16, R0 + 16 * KB, True)
    R1 = R0 + 24 * KB
    RING = R1 + 64 * KB
    ring = [P.sb([128, 16, 256], BF16, RING + 8 * KB * i, True) for i in range(3)]
    R5 = RING + 24 * KB
    hT_own = P.sb([128, 16, 1024], BF16, R5, True)
    hT_own_hi = P.mk(hT_own.t)
    R6 = R5 + 32 * KB
    kk0 = P.sb([128, 12 * 128], BF16, R6, True)
    kk1 = P.sb([128, 12 * 128], BF16, R6 + 3 * KB, True)
    vv = P.sb([128, 12, 256], BF16, R6 + 6 * KB, True)
    R6b = R6 + 12 * KB
    R7 = R6b + 16 * KB
    memT = P.sb([128, 16, 256], BF16, 199 * KB, True)
    memT_hi = P.mk(memT.t)

    wkv = P.sb([128, 16, 640], BF16, R1, True)
    gin = P.sb([128, D], F32, R1 + 20 * KB, True)
    xt = [P.sb([128, D], F32, R1 + 28 * KB + 8 * KB * i, True) for i in range(4)]
    hh = [P.sb([128, D], BF16, R1 + 60 * KB, True), P.sb([128, D], BF16, R6b, True),
          P.sb([128, D], BF16, R6b + 4 * KB, True)]
    hT_rot = [P.sb([128, 16, 256], BF16, R6b + 8 * KB, True), P.sb([128, 16, 256], BF16, R6b + 16 * KB, True)]
    hT_rot_hi = [P.mk(hT_rot[0].t), P.mk(hT_rot[1].t)]
    HI = {id(hT_rot[0]): hT_rot_hi[0], id(hT_rot[1]): hT_rot_hi[1], id(hT_own): hT_own_hi, id(memT): memT_hi}
    a_sc = R6b + 24 * KB
    sq = [P.sb([128, 2, 256], BF16, a_sc + KB * i, True) for i in range(2)]
    rstd2 = [P.sb([128, 256], F32, a_sc + 2 * KB + KB * i, True) for i in range(2)]
    tt_ = [P.sb([128, 256], F32, a_sc + 4 * KB, True)] * 2
    t2_ = P.sb([128, 256], F32, a_sc + 6 * KB, True)
    cst = [P.sb([128, 256], F32, a_sc + 7 * KB + KB * i, True) for i in range(2)]
    vvTs = [P.sb([128, 2, 256], BF16, a_sc + 9 * KB, True), P.sb([128, 2, 256], BF16, a_sc + 5 * KB, True)]
    gmem = P.sb([128, D], F32, 191 * KB, True)

    OUTS = [P.mk(None), P.mk(None)]

    def chk(n):
        if stop <= n:
            raise _Stop()

    def record():
        def wv(c0, c1):
            return w_in[:, c0:c1].rearrange("(kc p) c -> p kc c", p=128)
        CONA = P.mk(None)
        wkv_sw = P.mk(wkv.t)
        P.dma("pool", ident.t[:], ident_d[:, :], CONA)
        ident.w = P.totals(CONA)
        CONA2 = P.mk(None)
        for (dst, src, n) in [(0, C_CKV, 256), (256, C_KPE, 64), (320, C_KPE + 32, 32), (352, C_KPE, 32)]:
            P.dma("pool", wkv.t[:, :, dst:dst + n], wv(src, src + n), CONA2)
        wkv.w = P.totals(CONA2)

        conb_tot = []

        def load_conb():
            CONB = P.mk(None)
            P.dma("sp", gq.t[:], gq_d[:, :], CONB)
            P.dma("sp", kvb.t[:], kvb_d[:, :], CONB)
            P.dma("sp", gmem.t[:], gmem_d[:, :], CONB)
            for (dst, src, n) in [(384, C_KSWA, 128), (512, C_VSWA, 128)]:
                P.dma("pool", wkv.t[:, :, dst:dst + n], wv(src, src + n), CONB)
            P.dma("pool", kpeT.t[64:128, :], ohk[:, :], CONB)
            for b in (gq, kvb, gmem, wkv_sw):
                b.w = P.totals(CONB)
            kpeT.w = kpeT.w + P.totals(CONB)
            conb_tot.extend(P.totals(CONB))
        P.op("pool", lambda e: e.memset(ones.t[:], 1.0), writes=[ones])
        chk(1)

        ring_i = [0]

        def ring_load(src_ap):
            b = ring[ring_i[0] % 3]
            ring_i[0] += 1
            P.dma("pool", b.t[:], src_ap, b, writes=[b])
            return b

        NT = 38

        def tinfo(t):
            u, tl = t // 2, t % 2
            if u < 12:
                hTb, c0 = hT_rot[u % 2], 0
            elif u < 16:
                hTb, c0 = hT_own, (u - 12) * 256
            elif u < 18:
                hTb, c0 = hT_rot[u % 2], 0
            else:
                hTb, c0 = memT, 0
            if u < 18:
                src, gb = xp[u * 256 + tl * 128:u * 256 + (tl + 1) * 128, :], gin
            else:
                src, gb = memx[tl * 128:(tl + 1) * 128, :], gmem
            return u, tl, hTb, c0 + tl * 128, src, gb

        def S0(t):
            u, tl, hTb, c0, src, gb = tinfo(t)
            x_ = xt[t % 4]
            P.dma("sp", x_.t[:], src, x_, writes=[x_])

        def S1(t):
            x_, h_, sv = xt[t % 4], hh[t % 3], ssv[t % 3]
            P.op("act", lambda e: e.activation(out=h_.t[:], in_=x_.t[:], func=AF.Square, accum_out=sv.t[:, 0:1]),
                 reads=[x_], writes=[h_, sv])
            P.op("act", lambda e: e.activation(out=sv.t[:, 1:2], in_=sv.t[:, 0:1], func=AF.Ln, bias=EPS, scale=1.0 / D),
                 reads=[sv], writes=[sv])
            P.op("act", lambda e: e.activation(out=sv.t[:, 2:3], in_=sv.t[:, 1:2], func=AF.Exp, scale=-0.5),
                 reads=[sv], writes=[sv])

        def S2(t):
            u, tl, hTb, c0, src, gb = tinfo(t)
            x_, h_, sv = xt[t % 4], hh[t % 3], ssv[t % 3]
            P.op("dve", lambda e: e.scalar_tensor_tensor(out=h_.t[:], in0=x_.t[:], scalar=sv.t[:, 2:3], in1=gb.t[:],
                                                         op0=ALU.mult, op1=ALU.mult),
                 reads=[x_, sv, gb], writes=[h_])

        def S3(t):
            h_ = hh[t % 3]
            k = t % 2

            def tr(e):
                ins = None
                for kc in range(16):
                    ins = e.transpose(pb2(k, kc * 128, (kc + 1) * 128), h_.t[:, kc * 128:(kc + 1) * 128], ident.t[:])
                return ins
            P.op("pe", tr, reads=[h_, ident], writes=[PS[2 * k], PS[2 * k + 1]])

        def S4(t):
            u, tl, hTb, c0, src, gb = tinfo(t)
            k = t % 2
            P.op("act", lambda e: e.activation(out=hTb.t[:, 0:8, c0:c0 + 128],
                                               in_=pb2(k, 0, 1024).rearrange("p (a b) -> p a b", b=128), func=AF.Copy),
                 reads=[PS[2 * k]], writes=[hTb])
            P.op("dve", lambda e: e.tensor_copy(out=hTb.t[:, 8:16, c0:c0 + 128],
                                                in_=pb2(k, 1024, 2048).rearrange("p (a b) -> p a b", b=128)),
                 reads=[PS[2 * k + 1]], writes=[HI[id(hTb)]])

        def uinfo(u):
            if u < 12:
                return hT_rot[u % 2], 0
            if u < 16:
                return hT_own, (u - 12) * 256
            return hT_rot[u % 2], 0

        def U1(u):
            hTb, c0 = uinfo(u)
            p0, p1 = 4 + 2 * (u % 2), 5 + 2 * (u % 2)
            cs_ = cst[u % 2]
            P.dma("sp", cs_.t[:], csk[:, u * 256:(u + 1) * 256], cs_, writes=[cs_])

            def mm(e):
                ins = None
                for g in range(3):
                    dst = pf(p0, g * 256, (g + 1) * 256) if g < 2 else pf(p1, 0, 256)
                    for kc in range(16):
                        ins = e.matmul(dst, lhsT=wkv.t[:, kc, g * 128:(g + 1) * 128], rhs=hTb.t[:, kc, c0:c0 + 256],
                                       start=(kc == 0), stop=(kc == 15))
                return ins
            P.op("pe", mm, reads=[wkv, hTb, HI[id(hTb)]], writes=[PS[p0], PS[p1]])

        def U2(u):
            p0 = 4 + 2 * (u % 2)
            sq_ = sq[u % 2]
            P.op("act", lambda e: e.activation(out=sq_.t[:].rearrange("p a b -> p (a b)"), in_=pf(p0, 0, 512), func=AF.Square),
                 reads=[PS[p0]], writes=[sq_])

        def U3(u):
            p1 = 5 + 2 * (u % 2)
            sq_ = sq[u % 2]

            def mm2(e):
                e.matmul(pf(p1, 256, 512), lhsT=ones.t[:], rhs=sq_.t[:, 0, :], start=True, stop=False)
                return e.matmul(pf(p1, 256, 512), lhsT=ones.t[:], rhs=sq_.t[:, 1, :], start=False, stop=True)
            P.op("pe", mm2, reads=[ones, sq_], writes=[PS[p1]])

        def U4(u):
            p1 = 5 + 2 * (u % 2)
            r_ = rstd2[u % 2]
            P.op("act", lambda e: e.activation(out=r_.t[:], in_=pf(p1, 256, 512), func=AF.Ln, bias=EPS, scale=1.0 / 256),
                 reads=[PS[p1]], writes=[r_])
            P.op("act", lambda e: e.activation(out=r_.t[:], in_=r_.t[:], func=AF.Exp, scale=-0.5),
                 reads=[r_], writes=[r_])

        def U5(u):
            p0, p1 = 4 + 2 * (u % 2), 5 + 2 * (u % 2)
            r_, cs_, t_ = rstd2[u % 2], cst[u % 2], tt_[u % 2]
            for c in range(2):
                P.op("dve", lambda e, c=c: e.scalar_tensor_tensor(
                    out=kvnT.t[:, c, u * 256:(u + 1) * 256], in0=pf(p0, c * 256, (c + 1) * 256), scalar=gkv.t[:, c:c + 1],
                    in1=r_.t[:], op0=ALU.mult, op1=ALU.mult), reads=[PS[p0], gkv, r_], writes=[kvnT])
            P.op("dve", lambda e: e.tensor_tensor(out=t_.t[:], in0=pf(p1, 0, 256), in1=cs_.t[:], op=ALU.mult),
                 reads=[PS[p1], cs_], writes=[t_])
            P.op("pool", lambda e: e.tensor_copy(out=t2_.t[0:64, :], in_=t_.t[64:128, :]), reads=[t_], writes=[t2_])
            P.op("pool", lambda e: e.tensor_tensor(out=kpeT.t[0:64, u * 256:(u + 1) * 256], in0=t_.t[0:64, :],
                                                   in1=t2_.t[0:64, :], op=ALU.add), reads=[t_, t2_], writes=[kpeT])

        def SW(u):
            hTb, c0 = uinfo(u)
            p0, p1 = 4 + 2 * (u % 2), 5 + 2 * (u % 2)
            t0 = (u - 12) * 2 if u < 16 else 8 + (u - 16) * 2
            vvT = vvTs[u % 2]

            def mm3(e):
                ins = None
                for g in range(2):
                    dst = pf(p0 + g, 0, 256)
                    for kc in range(16):
                        ins = e.matmul(dst, lhsT=wkv.t[:, kc, (3 + g) * 128:(4 + g) * 128], rhs=hTb.t[:, kc, c0:c0 + 256],
                                       start=(kc == 0), stop=(kc == 15))
                return ins
            P.op("pe", mm3, reads=[wkv, wkv_sw, hTb, HI[id(hTb)]], writes=[PS[p0], PS[p1]])
            cs_ = slice(t0 * 128, (t0 + 2) * 128)
            for (dstb, r0) in ((kk0, 0), (kk1, 64)):
                for o0 in (0, 64):
                    P.op("dve", lambda e, dstb=dstb, r0=r0, o0=o0: e.tensor_copy(out=dstb.t[o0:o0 + 64, cs_], in_=pf(p0, 0, 256, r0, r0 + 64)),
                         reads=[PS[p0]], writes=[dstb])
            for (jj, r0) in ((0, 0), (1, 64)):
                for o0 in (0, 64):
                    P.op("act", lambda e, jj=jj, r0=r0, o0=o0: e.activation(out=vvT.t[o0:o0 + 64, jj, :], in_=pf(p1, 0, 256, r0, r0 + 64), func=AF.Copy),
                         reads=[PS[p1]], writes=[vvT])

        def SWb(u):
            p0 = 4 + 2 * (u % 2)
            t0 = (u - 12) * 2 if u < 16 else 8 + (u - 16) * 2
            vvT = vvTs[u % 2]

            def tr2(e):
                ins = None
                for tl in range(2):
                    for jj in range(2):
                        ins = e.transpose(pb(p0, (tl * 2 + jj) * 128, (tl * 2 + jj + 1) * 128),
                                          vvT.t[:, jj, tl * 128:(tl + 1) * 128], ident.t[:])
                return ins
            P.op("pe", tr2, reads=[vvT, ident], writes=[PS[p0]])
            P.op("dve", lambda e: e.tensor_copy(out=vv.t[:, t0:t0 + 2, :], in_=pb(p0, 0, 512).rearrange("p (a b) -> p a b", b=256)),
                 reads=[PS[p0]], writes=[vv])

        def unit_of(tlast):
            if tlast < 0 or tlast >= NT or tlast % 2 == 0:
                return None
            return tlast // 2

        def L(f, a):
            P.label = "%s(%d)" % (f.__name__, a)
            f(a)

        for k in range(-4, NT + 10):
            if 0 <= k + 4 < NT:
                L(S0, k + 4)
            if k == -3:
                P.label = "cong"
                CONG = P.mk(None)
                P.dma("sp", gin.t[:], gin_d[:, :], CONG)
                P.dma("sp", gkv.t[:], gkv_d[:, :], CONG)
                gin.w = P.totals(CONG)
                gkv.w = P.totals(CONG)
            if k == 8:
                P.label = "conb"
                load_conb()
            u = unit_of(k - 3)
            if u is not None and u < 16:
                L(U2, u)
            u = unit_of(k - 4)
            if u is not None and u < 16:
                L(U4, u)
                L(U5, u)
            if 0 <= k - 1 < NT:
                L(S4, k - 1)
            if 0 <= k + 2 < NT:
                L(S1, k + 2)
            if 0 <= k + 1 < NT:
                L(S2, k + 1)
            if 0 <= k < NT:
                L(S3, k)
            u = unit_of(k - 3)
            if u is not None and u < 16:
                L(U3, u)
            u = unit_of(k - 2)
            if u is not None and u < 16:
                L(U1, u)
            if u is not None and 16 <= u < 18:
                L(SW, u)
            u = unit_of(k - 3)
            if u is not None and 16 <= u < 18:
                L(SWb, u)
            u = unit_of(k - 4)
            if u is not None and 12 <= u < 16:
                L(SW, u)
            u = unit_of(k - 5)
            if u is not None and 12 <= u < 16:
                L(SWb, u)
        chk(4)
        P.label = "own"
        gate = P.sb([128, 16, 1024], BF16, R1)
        cqT = P.sb([128, 4, 1024], BF16, R1 + 32 * KB)
        wuq = P.sb([128, 4, 2048], BF16, R1 + 40 * KB)
        wukv = P.sb([128, 2, 2048], BF16, R1 + 56 * KB)
        qsw = P.sb([128, 4, 1024], BF16, R6b)
        qmem = P.sb([128, 4, 1024], BF16, R6b + 8 * KB)
        cqf = P.sb([128, 4, 512], F32, R7)
        sqq = P.sb([128, 4, 512], BF16, R7 + 8 * KB)
        lnq = P.sb([128, 512], F32, R7 + 12 * KB)

        otiles = [("cq", C_CQ), ("cq", C_CQ + 256)]
        otiles += [("z", C_ZMLA + 256 * i, 2 * i) for i in range(4)]
        otiles += [("qsw", C_QSWA, 0), ("qsw", C_QSWA + 256, 2)]
        otiles += [("z", C_ZSWA, 8), ("z", C_ZSWA + 256, 10)]
        otiles += [("qmem", C_QMEM, 0), ("qmem", C_QMEM + 256, 2)]
        otiles += [("z", C_ZMEM, 12), ("z", C_ZMEM + 256, 14)]
        pend = [ring_load(wv(t[1], t[1] + 256)) for t in otiles[:3]]
        chk(4.05)
        nxt = 3
        obank = [0]

        def proj_group(wb, g, half):
            b = obank[0] % 6
            obank[0] += 1

            def mm(e):
                ins = None
                for kc in range(16):
                    ins = e.matmul(pf(b, 0, 512), lhsT=wb.t[:, kc, g * 128:(g + 1) * 128],
                                   rhs=hT_own.t[:, kc, half * 512:(half + 1) * 512], start=(kc == 0), stop=(kc == 15))
                return ins
            P.op("pe", mm, reads=[wb, hT_own, hT_own_hi], writes=[PS[b]])
            return b

        wq0, wq1 = pend[0], pend[1]
        for half in range(2):
            for c in range(4):
                wb = wq0 if c < 2 else wq1
                b = proj_group(wb, c % 2, half)
                P.op("dve", lambda e, b=b, c=c: e.tensor_copy(out=cqf.t[:, c, :], in_=pf(b, 0, 512)), reads=[PS[b]], writes=[cqf])
                P.op("act", lambda e, c=c: e.activation(out=sqq.t[:, c, :], in_=cqf.t[:, c, :], func=AF.Square),
                     reads=[cqf], writes=[sqq])
                chk(4.1 + 0.01 * c + 0.04 * half)

            def mmq(e):
                ins = None
                for c in range(4):
                    ins = e.matmul(pf(6, 0, 512), lhsT=ones.t[:], rhs=sqq.t[:, c, :], start=(c == 0), stop=(c == 3))
                return ins
            P.op("pe", mmq, reads=[ones, sqq], writes=[PS[6]])
            chk(4.15 + 0.04 * half)
            P.op("act", lambda e: e.activation(out=lnq.t[:], in_=pf(6, 0, 512), func=AF.Ln, bias=EPS, scale=1.0 / 512),
                 reads=[PS[6]], writes=[lnq])
            P.op("act", lambda e: e.activation(out=lnq.t[:], in_=lnq.t[:], func=AF.Exp, scale=-0.5), reads=[lnq], writes=[lnq])
            for c in range(4):
                P.op("dve", lambda e, c=c, half=half: e.scalar_tensor_tensor(
                    out=cqT.t[:, c, half * 512:(half + 1) * 512], in0=cqf.t[:, c, :], scalar=gq.t[:, c:c + 1], in1=lnq.t[:],
                    op0=ALU.mult, op1=ALU.mult), reads=[cqf, gq, lnq], writes=[cqT])
        chk(4.2)
        pend = pend[2:]
        while len(pend) < 3 and nxt < len(otiles):
            t = otiles[nxt]; nxt += 1
            pend.append(ring_load(wv(t[1], t[1] + 256)))
        ev_i = [0]
        for ti in range(2, len(otiles)):
            kind, col, ch = otiles[ti]
            wb = pend.pop(0)
            for g in range(2):
                for half in range(2):
                    b = proj_group(wb, g, half)
                    hs = slice(half * 512, (half + 1) * 512)
                    if kind == "z":
                        P.op("act", lambda e, b=b, c=ch + g, hs=hs: e.activation(out=gate.t[:, c, hs], in_=pf(b, 0, 512), func=AF.Silu),
                             reads=[PS[b]], writes=[gate])
                    else:
                        dstb = qsw if kind == "qsw" else qmem
                        P.op("dve", lambda e, b=b, c=ch + g, hs=hs, dstb=dstb: e.tensor_copy(out=dstb.t[:, c, hs], in_=pf(b, 0, 512)),
                             reads=[PS[b]], writes=[dstb])
            chk(4.3 + 0.01 * ti)
            if nxt < len(otiles):
                t = otiles[nxt]; nxt += 1
                pend.append(ring_load(wv(t[1], t[1] + 256)))

        chk(5)
        WQ = P.mk(None)
        uqv = w_uq[:, :].rearrange("(kc p) (h c) -> p kc h c", p=128, c=192)
        wuqv = wuq.t[:].rearrange("p kc (h c) -> p kc h c", c=256)
        for kc in range(4):
            P.dma("pool", wuqv[:, kc, :, 0:192], uqv[:, kc, :, :], WQ, writes=[wuq])
            P.dma("pool", wuqv[:, kc, :, 192:224], uqv[:, kc, :, 160:192], WQ, writes=[wuq])
            P.dma("pool", wuqv[:, kc, :, 224:256], uqv[:, kc, :, 128:160], WQ, writes=[wuq])
        P.dma("pool", wukv.t[:], w_ukv[:, :].rearrange("(kc p) c -> p kc c", p=128), WQ, writes=[wukv])
        wuq.w = P.totals(WQ)
        wukv.w = P.totals(WQ)

        P.label = "swa"
        bm = P.sb([128, 2, 8, 128], F32, R5)
        swm = P.sb([128, 2, 128], F32, R5 + 8 * KB)
        ssw = [P.sb([128, 512], F32, R5 + 9 * KB + 2 * KB * i) for i in range(2)]
        psw = [P.sb([128, 512], BF16, R5 + 13 * KB + KB * i) for i in range(6)]
        sinkf = P.sb([1, 1024], F32, 187 * KB)
        sinkh = P.sb([1, 1024], BF16, 191 * KB)
        sinkl = P.sb([1, 1024], BF16, 193 * KB)
        sinkt = P.sb([1, 1024], F32, 195 * KB)
        SWB = P.mk(None)
        P.dma("sp", sinkf.t[:], sink_d[:, :], SWB, writes=[sinkf])
        P.dma("sp", bm.t[:].rearrange("p a b c -> p (a b c)"), swab_d[:, :], SWB, writes=[bm])
        P.dma("sp", swm.t[:].rearrange("p a b -> p (a b)"), swam_d[:, :], SWB, writes=[swm])
        bm.w = P.totals(SWB)
        swm.w = P.totals(SWB)
        sinkf.w = P.totals(SWB)
        P.label = "mem"
        kmT = P.sb([128, 4, 256], BF16, R7)
        vmem = P.sb([128, 2, 512], BF16, R7 + 2 * KB)
        pm = [P.sb([128, 2, 512], BF16, R7 + 4 * KB + 2 * KB * i) for i in range(2)]
        rinv = [P.sb([128, 512], F32, R7 + 8 * KB + 2 * KB * i) for i in range(2)]
        tmpo = P.sb([128, 512], F32, R7 + 12 * KB)
        mring = [ring_load(w_mkv[:, i * 256:(i + 1) * 256].rearrange("(kc p) c -> p kc c", p=128)) for i in range(3)]
        for i in range(4):
            wb = mring[i] if i < 3 else ring_load(w_mkv[:, 768:1024].rearrange("(kc p) c -> p kc c", p=128))
            if i < 2:
                for g in range(2):
                    hd = i * 2 + g
                    b = obank[0] % 6
                    obank[0] += 1

                    def mm(e, wb=wb, g=g, b=b):
                        ins = None
                        for kc in range(16):
                            ins = e.matmul(pf(b, 0, 256), lhsT=wb.t[:, kc, g * 128:(g + 1) * 128], rhs=memT.t[:, kc, :],
                                           start=(kc == 0), stop=(kc == 15))
                        return ins
                    P.op("pe", mm, reads=[wb, memT, memT_hi], writes=[PS[b]])
                    P.op("dve", lambda e, b=b, hd=hd: e.tensor_copy(out=kmT.t[:, hd, :], in_=pf(b, 0, 256)), reads=[PS[b]], writes=[kmT])
            else:
                b = obank[0] % 6
                obank[0] += 1

                def mm(e, wb=wb, b=b):
                    ins = None
                    for mt in range(2):
                        for kc in range(16):
                            ins = e.matmul(pf(b, mt * 256, (mt + 1) * 256), lhsT=memT.t[:, kc, mt * 128:(mt + 1) * 128],
                                           rhs=wb.t[:, kc, :], start=(kc == 0), stop=(kc == 15))
                    return ins
                P.op("pe", mm, reads=[wb, memT, memT_hi], writes=[PS[b]])
                cc = (i - 2) * 256
                P.op("dve", lambda e, b=b, cc=cc: e.tensor_copy(out=vmem.t[:, :, cc:cc + 256],
                                                                in_=pf(b, 0, 512).rearrange("p (a b) -> p a b", b=256)),
                     reads=[PS[b]], writes=[vmem])

        def finish(obk, lbk, ncol, pi, mixdst_fn):
            rv = rinv[pi % 2]
            P.op("act", lambda e: e.activation(out=rv.t[:, 0:ncol], in_=pf(lbk, 0, ncol), func=AF.Ln), reads=[PS[lbk]], writes=[rv])
            P.op("act", lambda e: e.activation(out=rv.t[:, 0:ncol], in_=rv.t[:, 0:ncol], func=AF.Exp, scale=-1.0), reads=[rv], writes=[rv])
            P.op("dve", lambda e: e.tensor_tensor(out=tmpo.t[:, 0:ncol], in0=pf(obk, 0, ncol), in1=rv.t[:, 0:ncol], op=ALU.mult),
                 reads=[PS[obk], rv], writes=[tmpo])
            mixdst_fn()

        def memA(it):
            hd, half = it // 2, it % 2
            hs = slice(half * 512, (half + 1) * 512)
            sb_ = [(it % 2) * 2, (it % 2) * 2 + 1]
            pmb = pm[it % 2]

            def mms(e):
                ins = None
                for mt in range(2):
                    ins = e.matmul(pf(sb_[mt], 0, 512), lhsT=kmT.t[:, hd, mt * 128:(mt + 1) * 128], rhs=qmem.t[:, hd, hs],
                                   start=True, stop=True)
                return ins
            P.op("pe", mms, reads=[kmT, qmem], writes=[PS[sb_[0]], PS[sb_[1]]])
            for mt in range(2):
                P.op("act", lambda e, mt=mt: e.activation(out=pmb.t[:, mt, :], in_=pf(sb_[mt], 0, 512),
                                                          func=AF.Exp, scale=MEM_SCALE),
                     reads=[PS[sb_[mt]]], writes=[pmb])

        def memB(it):
            hd, half = it // 2, it % 2
            hs = slice(half * 512, (half + 1) * 512)
            ob, lb = 4 + (it % 2), 6 + (it % 2)
            pmb = pm[it % 2]

            def mmo(e):
                for mt in range(2):
                    e.matmul(pf(ob, 0, 512), lhsT=vmem.t[:, mt, hd * 128:(hd + 1) * 128], rhs=pmb.t[:, mt, :],
                             start=(mt == 0), stop=(mt == 1))
                ins = None
                for mt in range(2):
                    ins = e.matmul(pf(lb, 0, 512), lhsT=ones.t[:], rhs=pmb.t[:, mt, :], start=(mt == 0), stop=(mt == 1))
                return ins
            P.op("pe", mmo, reads=[vmem, pmb, ones], writes=[PS[ob], PS[lb]])

            def mixm():
                P.op("pool", lambda e: e.tensor_tensor(out=gate.t[:, 12 + hd, hs], in0=tmpo.t[:], in1=gate.t[:, 12 + hd, hs], op=ALU.mult),
                     reads=[tmpo, gate], writes=[gate])
            finish(ob, lb, 512, it, mixm)

        for k in range(9):
            if k < 8:
                memA(k)
            if k >= 1:
                memB(k - 1)

        chk(6)
        P.label = "swa"
        for kt in range(2):
            P.op("pool", lambda e, kt=kt: e.tensor_tensor(out=bm.t[:, kt, :, :], in0=bm.t[:, kt, :, :],
                                                          in1=swm.t[:, kt:kt + 1, :].to_broadcast([128, 8, 128]), op=ALU.add),
                 reads=[bm, swm], writes=[bm])
        P.op("act", lambda e: e.activation(out=sinkt.t[:], in_=sinkf.t[:], func=AF.Exp), reads=[sinkf], writes=[sinkt])
        P.op("dve", lambda e: e.tensor_copy(out=sinkh.t[:], in_=sinkt.t[:]), reads=[sinkt], writes=[sinkh])
        P.op("dve", lambda e: e.tensor_copy(out=sinkf.t[:], in_=sinkh.t[:]), reads=[sinkh], writes=[sinkf])
        P.op("dve", lambda e: e.tensor_tensor(out=sinkl.t[:], in0=sinkt.t[:], in1=sinkf.t[:], op=ALU.subtract),
             reads=[sinkt, sinkf], writes=[sinkl])

        def swinfo(it):
            qb, kvh = it // 2, it % 2
            ui = qb // 2
            tiles = [(8 + ui) if qb % 2 == 0 else (qb - 1), qb]
            return qb, kvh, tiles, (kk0 if kvh == 0 else kk1)

        def swaA(it, part):
            qb, kvh, tiles, kk = swinfo(it)

            def mms(e):
                ins = None
                for kt in range(2):
                    for hq in range(4):
                        hd = kvh * 4 + hq
                        par, hq2 = hq % 2, hq // 2
                        g, pr = hd // 2, par * 64
                        ins = e.matmul(pf(kt * 2 + par, hq2 * 128, (hq2 + 1) * 128),
                                       lhsT=kk.t[pr:pr + 64, tiles[kt] * 128:(tiles[kt] + 1) * 128],
                                       rhs=qsw.t[pr:pr + 64, g, qb * 128:(qb + 1) * 128], start=True, stop=True)
                return ins
            if part == 1:
                P.op("pe", mms, reads=[kk, qsw], writes=[PS[0], PS[1], PS[2], PS[3]])
            for kt in range(2):
                sw_ = ssw[kt]
                pb_ = psw[(it % 3) * 2 + kt]
                if part == 1:
                    P.op("dve", lambda e, kt=kt, sw_=sw_: e.scalar_tensor_tensor(
                        out=sw_.t[:].rearrange("p (a b) -> p a b", a=2),
                        in0=DB[kt][:, :].rearrange("p (b c) -> p b c", b=2)[:, :, 0:256], scalar=0.125,
                        in1=bm.t[:, kt, kvh * 4:(kvh + 1) * 4, :].rearrange("p (a b) c -> p a (b c)", a=2), op0=ALU.mult, op1=ALU.add),
                        reads=[PS[kt * 2], PS[kt * 2 + 1], bm], writes=[sw_])
                    continue
                if kt == 0:
                    P.op("act", lambda e, sw_=sw_, pb_=pb_: e.activation(out=pb_.t[:], in_=sw_.t[:], func=AF.Exp,
                                                                          bias=kvb.t[:, qb:qb + 1], scale=1.0),
                         reads=[sw_, kvb], writes=[pb_])
                else:
                    P.op("act", lambda e, sw_=sw_, pb_=pb_: e.activation(out=pb_.t[:], in_=sw_.t[:], func=AF.Exp),
                         reads=[sw_], writes=[pb_])

        def swaB(it):
            qb, kvh, tiles, kk = swinfo(it)
            ob, lb = 4 + (it % 2), 6 + (it % 2)
            pbs = [psw[(it % 3) * 2 + kt] for kt in range(2)]

            def mmo(e):
                for kt in range(2):
                    e.matmul(pf(ob, 0, 512), lhsT=vv.t[:, tiles[kt], kvh * 128:(kvh + 1) * 128], rhs=pbs[kt].t[:],
                             start=(kt == 0), stop=(kt == 1))
                for kt in range(2):
                    e.matmul(pf(lb, 0, 512), lhsT=ones.t[:], rhs=pbs[kt].t[:], start=(kt == 0), stop=False)
                e.matmul(pf(lb, 0, 512), lhsT=ones.t[0:1, :], rhs=sinkh.t[0:1, kvh * 512:(kvh + 1) * 512], start=False, stop=False)
                return e.matmul(pf(lb, 0, 512), lhsT=ones.t[0:1, :], rhs=sinkl.t[0:1, kvh * 512:(kvh + 1) * 512], start=False, stop=True)
            P.op("pe", mmo, reads=[vv, ones, sinkh, sinkl] + pbs, writes=[PS[ob], PS[lb]])

            def mixs():
                for par in range(2):
                    pr = par * 64
                    c8 = 8 + kvh * 2
                    P.op("pool", lambda e, par=par, pr=pr, c8=c8: e.tensor_tensor(
                        out=gate.t[pr:pr + 64, c8:c8 + 2, qb * 128:(qb + 1) * 128],
                        in0=tmpo.t[pr:pr + 64, par * 256:(par + 1) * 256].rearrange("p (a b) -> p a b", a=2),
                        in1=gate.t[pr:pr + 64, c8:c8 + 2, qb * 128:(qb + 1) * 128], op=ALU.mult), reads=[tmpo, gate], writes=[gate])
            finish(ob, lb, 512, it, mixs)

        for k in range(18):
            if k < 16:
                swaA(k, 1)
            if k >= 2:
                swaB(k - 2)
            if k < 16:
                swaA(k, 2)

        chk(7)
        P.label = "mla"
        Kb = [P.sb([128, S], BF16, R5 + 8 * KB * i) for i in range(2)]
        Vb = [P.sb([128, 32, 128], BF16, R5 + 16 * KB + 8 * KB * i) for i in range(2)]
        Qn = [P.sb([128, 1024], BF16, R6 + 2 * KB * i) for i in range(2)]
        Qp = [P.sb([128, 1024], BF16, R6 + 4 * KB + 2 * KB * i) for i in range(2)]
        csq = P.sb([128, 1024], F32, R6 + 8 * KB)
        tq = P.sb([128, 512], F32, R6 + 12 * KB)
        tq2 = P.sb([128, 512], F32, R6 + 14 * KB)
        pmla = [P.sb([128, 512], BF16, R6 + 16 * KB + KB * i) for i in range(3)]
        p2s = [P.sb([128, 256], BF16, R6 + 22 * KB + 512 * i) for i in range(3)]
        rinvm = [P.sb([128, 256], F32, R6 + 19 * KB + KB * i) for i in range(2)]
        tmpm = P.sb([128, 256], F32, R6 + 21 * KB)
        QC = P.mk(None)
        P.dma("sp", csq.t[:], csk[:, 3072:4096], QC, writes=[csq])
        for i in range(2):
            P.dma("pool", Qp[i].t[64:128, :], gtm[:, :], QC, writes=[Qp[i]])
        for b_ in (csq, Qp[0], Qp[1]):
            b_.w = b_.w + P.totals(QC)

        GEN_BANK = 7

        def gen_chunks(h, banks=(7,)):
            s = h % 2
            K_, V_, Qn_, Qp_ = Kb[s], Vb[s], Qn[s], Qp[s]
            ch = []
            for half in range(2):
                gb = banks[len(ch) % len(banks)]
                def qn(half=half, gb=gb):
                    def mm(e):
                        ins = None
                        for kc in range(4):
                            ins = e.matmul(pf(gb, 0, 512), lhsT=wuq.t[:, kc, h * 256:h * 256 + 128],
                                           rhs=cqT.t[:, kc, half * 512:(half + 1) * 512], start=(kc == 0), stop=(kc == 3))
                        return ins
                    P.op("pe", mm, reads=[wuq, cqT], writes=[PS[gb]])
                    P.op("dve", lambda e: e.tensor_copy(out=Qn_.t[:, half * 512:(half + 1) * 512], in_=pf(gb, 0, 512)),
                         reads=[PS[gb]], writes=[Qn_])
                ch.append(qn)

                gb = banks[len(ch) % len(banks)]
                def qp(half=half, gb=gb):
                    def mm(e):
                        ins = None
                        for kc in range(4):
                            ins = e.matmul(pf(gb, 0, 512), lhsT=wuq.t[:, kc, h * 256 + 128:h * 256 + 256],
                                           rhs=cqT.t[:, kc, half * 512:(half + 1) * 512], start=(kc == 0), stop=(kc == 3))
                        return ins
                    P.op("pe", mm, reads=[wuq, cqT], writes=[PS[gb]])
                    P.op("dve", lambda e: e.tensor_tensor(out=tq.t[:], in0=pf(gb, 0, 512), in1=csq.t[:, half * 512:(half + 1) * 512], op=ALU.mult),
                         reads=[PS[gb], csq], writes=[tq])
                    P.op("pool", lambda e: e.tensor_copy(out=tq2.t[0:64, :], in_=tq.t[64:128, :]), reads=[tq], writes=[tq2])
                    P.op("pool", lambda e: e.tensor_tensor(out=Qp_.t[0:64, half * 512:(half + 1) * 512], in0=tq.t[0:64, :], in1=tq2.t[0:64, :], op=ALU.add),
                         reads=[tq, tq2], writes=[Qp_])
                ch.append(qp)
            for c8 in range(8):
                gb = banks[len(ch) % len(banks)]
                def kg(c8=c8, gb=gb):
                    def mm(e):
                        ins = None
                        for kc in range(2):
                            ins = e.matmul(pf(gb, 0, 512), lhsT=wukv.t[:, kc, h * 256:h * 256 + 128],
                                           rhs=kvnT.t[:, kc, c8 * 512:(c8 + 1) * 512], start=(kc == 0), stop=(kc == 1))
                        return ins
                    P.op("pe", mm, reads=[wukv, kvnT], writes=[PS[gb]])
                    P.op("dve", lambda e: e.tensor_copy(out=K_.t[:, c8 * 512:(c8 + 1) * 512], in_=pf(gb, 0, 512)),
                         reads=[PS[gb]], writes=[K_])
                ch.append(kg)

                gb = banks[len(ch) % len(banks)]
                def vg(c8=c8, gb=gb):
                    def mm(e):
                        ins = None
                        for t4 in range(4):
                            tl = c8 * 4 + t4
                            for kc in range(2):
                                ins = e.matmul(pf(gb, t4 * 128, (t4 + 1) * 128), lhsT=kvnT.t[:, kc, tl * 128:(tl + 1) * 128],
                                               rhs=wukv.t[:, kc, h * 256 + 128:h * 256 + 256], start=(kc == 0), stop=(kc == 1))
                        return ins
                    P.op("pe", mm, reads=[wukv, kvnT], writes=[PS[gb]])
                    P.op("dve", lambda e: e.tensor_copy(out=V_.t[:, c8 * 4:(c8 + 1) * 4, :], in_=pf(gb, 0, 512).rearrange("p (a b) -> p a b", b=128)),
                         reads=[PS[gb]], writes=[V_])
                ch.append(vg)
            return ch

        for c in gen_chunks(0, banks=(0, 1, 2, 3, 4, 5, 6, 7)):
            c()

        kpeT.w = kpeT.w + conb_tot
        P.label = "outpre"
        wo_t = [ring_load(w_out[:, i * 256:(i + 1) * 256].rearrange("(kc p) c -> p kc c", p=128)) for i in range(3)]
        P.label = "mla"
        steps = []
        for h in range(8):
            for i in range(4):
                kus = list(range(3 * (i + 1))) + list(range(12, 13 + i))
                for n, ku in enumerate(kus):
                    steps.append((h, i, ku, n == 0, n == len(kus) - 1))
        SKEW = 2
        pendq = []
        sidx = [0]
        gen_next = []
        cur_h = -1

        def rec_pv(item):
            (h, i, ku, first, last, pbuf, p2) = item
            s = h % 2
            ob, lb = 3 + (i % 2), 5 + (i % 2)

            def mm(e):
                for t in range(2):
                    tl = ku * 2 + t
                    e.matmul(pf(ob, 0, 256), lhsT=Vb[s].t[:, tl, :], rhs=pbuf.t[:, t * 256:(t + 1) * 256],
                             start=(first and t == 0), stop=(last and t == 1))
                return e.matmul(pf(lb, 0, 256), lhsT=ones.t[:], rhs=p2.t[:], start=first, stop=last)
            P.op("pe", mm, reads=[Vb[s], pbuf, p2, ones], writes=[PS[ob], PS[lb]])
            if last:
                rv = rinvm[i % 2]
                P.op("act", lambda e: e.activation(out=rv.t[:], in_=pf(lb, 0, 256), func=AF.Ln), reads=[PS[lb]], writes=[rv])
                P.op("act", lambda e: e.activation(out=rv.t[:], in_=rv.t[:], func=AF.Exp, scale=-1.0), reads=[rv], writes=[rv])
                P.op("dve", lambda e: e.tensor_tensor(out=tmpm.t[:], in0=pf(ob, 0, 256), in1=rv.t[:], op=ALU.mult),
                     reads=[PS[ob], rv], writes=[tmpm])
                P.op("pool", lambda e: e.tensor_tensor(out=gate.t[:, h, i * 256:(i + 1) * 256], in0=tmpm.t[:],
                                                       in1=gate.t[:, h, i * 256:(i + 1) * 256], op=ALU.mult),
                     reads=[tmpm, gate], writes=[gate])

        for (h, i, ku, first, last) in steps:
            if h != cur_h:
                assert len(pendq) <= SKEW
                cur_h = h
                gen_next = gen_chunks(h + 1) if h + 1 < 8 else []
            s = h % 2
            sbank = sidx[0] % 3
            pbuf = pmla[sidx[0] % 3]
            sidx[0] += 1

            def mms(e, s=s, i=i, ku=ku, sbank=sbank):
                ins = None
                for t in range(2):
                    kc0 = ku * 256 + t * 128
                    e.matmul(pf(sbank, t * 256, (t + 1) * 256), lhsT=Kb[s].t[:, kc0:kc0 + 128], rhs=Qn[s].t[:, i * 256:(i + 1) * 256],
                             start=True, stop=False)
                    ins = e.matmul(pf(sbank, t * 256, (t + 1) * 256), lhsT=kpeT.t[:, kc0:kc0 + 128], rhs=Qp[s].t[:, i * 256:(i + 1) * 256],
                                   start=False, stop=True)
                return ins
            P.op("pe", mms, reads=[Kb[s], Qn[s], Qp[s], kpeT], writes=[PS[sbank]])
            P.op("act", lambda e, sbank=sbank, pbuf=pbuf: e.activation(out=pbuf.t[:], in_=pf(sbank, 0, 512), func=AF.Exp, scale=MLA_SCALE),
                 reads=[PS[sbank]], writes=[pbuf])
            p2 = p2s[(sidx[0] - 1) % 3]
            P.op("dve", lambda e, pbuf=pbuf, p2=p2: e.tensor_tensor(out=p2.t[:], in0=pbuf.t[:, 0:256], in1=pbuf.t[:, 256:512], op=ALU.add),
                 reads=[pbuf], writes=[p2])
            pendq.append((h, i, ku, first, last, pbuf, p2))
            if len(pendq) > SKEW:
                rec_pv(pendq.pop(0))
            if gen_next and (sidx[0] % 2 == 0):
                gen_next.pop(0)()
        while pendq:
            rec_pv(pendq.pop(0))
        assert not gen_next

        chk(8)
        P.label = "out"
        res = P.sb([128, 8, D], F32, R5)
        resb = [P.mk(res.t, fresh=False) for _ in range(8)]
        gfin = P.sb([128, D], F32, 177 * KB)
        xo = ([P.sb([128, D], F32, R1 + 32 * KB + 8 * KB * i) for i in range(4)]
              + [P.sb([128, D], F32, 185 * KB), P.sb([128, D], F32, 193 * KB)]
              + [P.sb([128, D], F32, 1 * KB), P.sb([128, D], F32, 9 * KB)])
        GF = P.mk(None)
        for tt in range(8):
            P.dma("sp", xo[tt].t[:], xp[3072 + tt * 128:3072 + (tt + 1) * 128, :], xo[tt], writes=[xo[tt]])
        P.dma("sp", gfin.t[:], gfin_d[:, :], GF, writes=[gfin])
        obk = [0]
        NFUSE = 2

        def F1(tt):
            x_, r_ = xo[tt], resb[tt]
            sv = ssv[tt % 4]
            P.op("dve", lambda e: e.tensor_tensor(out=res.t[:, tt, 0:NFUSE * 256], in0=res.t[:, tt, 0:NFUSE * 256],
                                                  in1=x_.t[:, 0:NFUSE * 256], op=ALU.add),
                 reads=[x_], writes=[r_])
            P.op("act", lambda e: e.activation(out=x_.t[:], in_=res.t[:, tt, :], func=AF.Square, accum_out=sv.t[:, 0:1]),
                 reads=[r_], writes=[x_, sv])
            P.op("act", lambda e: e.activation(out=sv.t[:, 1:2], in_=sv.t[:, 0:1], func=AF.Ln, bias=EPS, scale=1.0 / D),
                 reads=[sv], writes=[sv])
            P.op("act", lambda e: e.activation(out=sv.t[:, 2:3], in_=sv.t[:, 1:2], func=AF.Exp, scale=-0.5),
                 reads=[sv], writes=[sv])

        def F2(tt):
            r_ = resb[tt]
            sv = ssv[tt % 4]
            P.op("dve", lambda e: e.scalar_tensor_tensor(out=res.t[:, tt, :], in0=res.t[:, tt, :], scalar=sv.t[:, 2:3], in1=gfin.t[:],
                                                         op0=ALU.mult, op1=ALU.mult),
                 reads=[sv, gfin], writes=[r_])
            P.dma("sp", out_d[tt * 128:(tt + 1) * 128, :], res.t[:, tt, :], OUTS[tt % 2], reads=[r_])

        def fin_items(t0, n=4):
            items = []
            for k in range(n + 1):
                it = []
                if k < n:
                    it.append((F1, t0 + k))
                if k >= 1:
                    it.append((F2, t0 + k - 1))
                items.append(it)
            return items

        nload = [3]
        for ci in range(8):
            P.label = "out"
            wb = wo_t.pop(0)
            for tt in range(8):
                b = obk[0] % 8
                obk[0] += 1

                def mm(e, wb=wb, tt=tt, b=b):
                    ins = None
                    for kc in range(16):
                        ins = e.matmul(pf(b, 0, 256), lhsT=gate.t[:, kc, tt * 128:(tt + 1) * 128], rhs=wb.t[:, kc, :],
                                       start=(kc == 0), stop=(kc == 15))
                    return ins
                P.op("pe", mm, reads=[wb, gate], writes=[PS[b]])
                cs_ = slice(ci * 256, (ci + 1) * 256)
                if ci >= NFUSE:
                    P.op("dve", lambda e, b=b, tt=tt, cs_=cs_: e.tensor_tensor(out=res.t[:, tt, cs_], in0=pf(b, 0, 256),
                                                                               in1=xo[tt].t[:, cs_], op=ALU.add),
                         reads=[PS[b], xo[tt]], writes=[resb[tt]])
                elif tt % 2 == 0:
                    P.op("act", lambda e, b=b, tt=tt, cs_=cs_: e.activation(out=res.t[:, tt, cs_], in_=pf(b, 0, 256), func=AF.Copy),
                         reads=[PS[b]], writes=[resb[tt]])
                else:
                    P.op("dve", lambda e, b=b, tt=tt, cs_=cs_: e.tensor_copy(out=res.t[:, tt, cs_], in_=pf(b, 0, 256)),
                         reads=[PS[b]], writes=[resb[tt]])
            if nload[0] < 8:
                cn = nload[0] % 8
                nload[0] += 1
                wo_t.append(ring_load(w_out[:, cn * 256:(cn + 1) * 256].rearrange("(kc p) c -> p kc c", p=128)))
        P.label = "fin"
        for it in fin_items(0, 8):
            for f, a in it:
                f(a)
    try:
        record()
    except _Stop:
        P.dma("sp", out_d[0:128, :], kvnT.t[:, 0, 0:1024].bitcast(F32).rearrange("p (a b) -> p a b", a=1)[:, 0, :] if False else kvnT.t[:].rearrange("p a b -> p (a b)").bitcast(F32)[:, 0:2048], OUTS[0], reads=[kvnT, kpeT])
    fin = []
    for ob_ in OUTS:
        fin += P.totals(ob_)

    blk = es.enter_context(nc.Block())

    @blk.sync
    def _(e):
        P.replay("sp", e)
        for k, v in fin:
            e.wait_ge(P.sem[k], v)

    @blk.gpsimd
    def _(e):
        P.replay("pool", e)

    @blk.scalar
    def _(e):
        P.replay("act", e)

    @blk.vector
    def _(e):
        P.replay("dve", e)

    @blk.tensor
    def _(e):
        P.replay("pe", e)

    es.close()
    nc._prog_labels = P.labels
    return nc


def _t5_bucket(rel):
    nb = 16
    max_exact = 8
    bucket = np.where(rel > 0, nb, 0)
    n = np.abs(rel)
    nf = np.maximum(n, 1).astype(np.float32)
    large = max_exact + (np.log(nf / max_exact) / np.log(128 / max_exact) * (nb - max_exact)).astype(np.int32)
    large = np.minimum(large, nb - 1)
    return bucket + np.where(n < max_exact, n, large)


def _own_units(j):
    return [j, 7 - j, 8 + j, 15 - j]


_NC_CACHE = {}


def _prepare(x, mem, norm_in, w_in, norm_q, norm_kv, w_uq, w_ukv, attn_sinks, rel_bias,
             norm_mem, w_mem_kv, w_out, norm_final):
    f32 = np.float32
    x = np.asarray(x, f32); mem = np.asarray(mem, f32)
    w_in0 = np.ascontiguousarray(np.asarray(w_in, f32)[0])
    w_uq0 = np.ascontiguousarray(np.asarray(w_uq, f32)[0])
    w_ukv0 = np.ascontiguousarray(np.asarray(w_ukv, f32)[0])
    w_mkv0 = np.ascontiguousarray(np.asarray(w_mem_kv, f32)[0])
    w_out0 = np.ascontiguousarray(np.asarray(w_out, f32)[0])
    bc = lambda v: np.ascontiguousarray(np.broadcast_to(np.asarray(v, f32).reshape(1, -1), (128, np.asarray(v).size)))
    gin = bc(norm_in[0]); gmem = bc(norm_mem[0]); gfin = bc(norm_final)
    gq = np.ascontiguousarray(np.asarray(norm_q, f32)[0].reshape(4, 128).T)
    gkv = np.ascontiguousarray(np.asarray(norm_kv, f32)[0].reshape(2, 128).T)
    inv = (1.0 / (np.float32(10000.0) ** (np.arange(0, 64, 2, dtype=f32) / np.float32(64)))).astype(f32)
    ang = (np.arange(S, dtype=f32)[:, None] * inv[None, :]).astype(f32)
    cos = np.cos(ang).astype(f32).T
    sin = np.sin(ang).astype(f32).T
    cs_nat = np.concatenate([cos, cos, -sin, sin], axis=0)
    qi = np.arange(128); kj = np.arange(256)
    rel = kj[None, :] - 128 - qi[:, None]
    bidx = _t5_bucket(rel)
    rb = np.asarray(rel_bias, f32)
    bias_qkh = rb[bidx]
    hperm = [kvh * 4 + hq2 * 2 + par for kvh in range(2) for par in range(2) for hq2 in range(2)]
    swab = np.ascontiguousarray(bias_qkh[:, :, hperm].transpose(1, 2, 0).reshape(2, 128, 8, 128).transpose(1, 0, 2, 3)).reshape(128, 2048)
    dq = (kj[None, :] // 64) - (qi[:, None] // 64)
    valid = (dq >= 0) & (dq <= 2)
    swam = np.where(valid, 0.0, -BIG).astype(f32)
    swam = np.ascontiguousarray(swam.T.reshape(2, 128, 128).transpose(1, 0, 2)).reshape(128, 256)
    sinkr = np.ascontiguousarray(np.repeat(np.asarray(attn_sinks, f32)[0][hperm], 128).reshape(1, 1024))
    ident = np.eye(128, dtype=f32)

    in_maps = []
    meta = []
    for core in range(8):
        b, j = core // 4, core % 4
        own = _own_units(j)
        others = [u for u in range(16) if u not in own]
        order = others + own
        tok = np.concatenate([np.arange(u * 256, (u + 1) * 256) for u in order])
        own_tok = tok[3072:]
        xb = x[b]
        pred = np.zeros((512, D), f32)
        kvbv = np.zeros((128, 8), f32)
        for i, u in enumerate(own):
            if u > 0:
                pred[i * 128:(i + 1) * 128] = xb[u * 256 - 128:u * 256]
            else:
                kvbv[:, 2 * i] = -BIG
        xpm = np.concatenate([xb[tok], pred], axis=0)
        chunk = tok // 64
        ohk = np.zeros((64, S), f32)
        ohk[chunk, np.arange(S)] = 1.0
        qchunk = own_tok // 64
        gt = np.where(np.arange(64)[:, None] > qchunk[None, :], -BIG, 0.0).astype(f32)
        in_maps.append({
            "xp": np.ascontiguousarray(xpm), "memx": np.ascontiguousarray(mem[b]),
            "csk": np.ascontiguousarray(cs_nat[:, tok]), "ohk": ohk, "gtm": gt,
            "w_in": w_in0, "w_uq": w_uq0, "w_ukv": w_ukv0, "w_mkv": w_mkv0, "w_out": w_out0,
            "gin": gin, "gmem": gmem, "gfin": gfin, "gq": gq, "gkv": gkv,
            "swab": swab, "swam": swam, "kvb": kvbv, "sinkr": sinkr, "ident": ident,
        })
        meta.append((b, own_tok))
    return in_maps, meta


def kernel(x, mem, norm_in, w_in, norm_q, norm_kv, w_uq, w_ukv, attn_sinks, rel_bias,
           norm_mem, w_mem_kv, w_out, norm_final):
    in_maps, meta = _prepare(x, mem, norm_in, w_in, norm_q, norm_kv, w_uq, w_ukv, attn_sinks, rel_bias,
                             norm_mem, w_mem_kv, w_out, norm_final)
    f32 = np.float32
    if "nc" not in _NC_CACHE:
        _NC_CACHE["nc"] = build()
    res = run_bass_kernel_spmd(_NC_CACHE["nc"], in_maps, core_ids=list(range(8)))
    out = np.zeros((NB, S, D), f32)
    for core in range(8):
        b, own_tok = meta[core]
        out[b, own_tok] = res.results[core]["out"]
    return out
```
